# Optimizing a Trainium2 kernel written in Bass

```python
import jax, jax.numpy as jnp
from jax import lax
import numpy as np

D_MODEL = 1024
BATCH = 2
SEQ = 8192
DEPTH = 1

HG_HEADS = 8
HG_EXPAND = 128
HG_KEY_DIM = HG_HEADS * HG_EXPAND
HG_VAL_DIM = D_MODEL
HG_HEAD_V = HG_VAL_DIM // HG_HEADS
HG_CHUNK = 64
ATT_Q_HEADS = 16
ATT_KV_HEADS = 4
ATT_HEAD_DIM = 64
ATT_GROUP = ATT_Q_HEADS // ATT_KV_HEADS
ATT_WIDTH = ATT_Q_HEADS * ATT_HEAD_DIM
KV_WIDTH = ATT_KV_HEADS * ATT_HEAD_DIM
WINDOW = 128
ATT_BLOCK = WINDOW
EPS = 1e-6

IN_SIZES = [
    HG_KEY_DIM,
    HG_KEY_DIM,
    HG_VAL_DIM,
    HG_VAL_DIM,
    ATT_WIDTH,
    KV_WIDTH,
    KV_WIDTH,
    ATT_WIDTH,
    D_MODEL,
    D_MODEL,
]
D_IN = sum(IN_SIZES)
SPLIT_POINTS = [int(s) for s in np.cumsum(IN_SIZES)[:-1]]

kernel_name = "hybrid_hgrn2_swa_sink_gated_block"


def rms_norm(x, w):
    xf = x.astype(jnp.float32)
    xf = xf * lax.rsqrt(jnp.mean(xf * xf, axis=-1, keepdims=True) + EPS)
    return xf.astype(x.dtype) * w


def hgrn2_chunkwise(q, k, v, log_f):
    B, T, H, dk = q.shape
    dv = v.shape[-1]
    n = T // HG_CHUNK

    def to_chunks(a):
        return a.reshape(B, n, HG_CHUNK, H, a.shape[-1]).transpose(1, 0, 3, 2, 4)

    qc, kc, vc, gc = to_chunks(q), to_chunks(k), to_chunks(v), to_chunks(log_f)
    causal = jnp.tril(jnp.ones((HG_CHUNK, HG_CHUNK), dtype=bool))

    def step(S, inp):
        q_, k_, v_, g_ = inp
        b = jnp.cumsum(g_, axis=2)
        o_inter = jnp.einsum('bhtd,bhdv->bhtv', q_ * jnp.exp(b), S)
        diff = b[:, :, :, None, :] - b[:, :, None, :, :]
        decay = jnp.exp(jnp.where(causal[:, :, None], diff, -jnp.inf))
        scores = jnp.einsum('bhtd,bhsd,bhtsd->bhts', q_, k_, decay)
        o_intra = jnp.einsum('bhts,bhsv->bhtv', scores, v_)
        b_last = b[:, :, -1:, :]
        S_new = jnp.exp(b_last[:, :, 0, :])[..., None] * S + jnp.einsum(
            'bhsd,bhsv->bhdv', k_ * jnp.exp(b_last - b), v_)
        return S_new, o_inter + o_intra

    S0 = jnp.zeros((B, H, dk, dv), jnp.float32)
    _, o = lax.scan(step, S0, (qc, kc, vc, gc))
    return o.transpose(1, 0, 3, 2, 4).reshape(B, T, H, dv)


def sliding_window_attention_with_sinks(q, k, v, sinks):
    B, T = q.shape[0], q.shape[1]
    n = T // ATT_BLOCK
    qb = q.reshape(B, n, ATT_BLOCK, ATT_KV_HEADS, ATT_GROUP, ATT_HEAD_DIM)

    def banded(a):
        ab = a.reshape(B, n, ATT_BLOCK, ATT_KV_HEADS, ATT_HEAD_DIM)
        prev = jnp.pad(ab[:, :-1], ((0, 0), (1, 0), (0, 0), (0, 0), (0, 0)))
        return jnp.concatenate([prev, ab], axis=2)

    keys, vals = banded(k), banded(v)
    scores = jnp.einsum('bnqhgd,bnkhd->bnhgqk', qb, keys).astype(jnp.float32) * (ATT_HEAD_DIM ** -0.5)
    qi = jnp.arange(ATT_BLOCK)[:, None]
    kj = jnp.arange(2 * ATT_BLOCK)[None, :]
    rel = qi + ATT_BLOCK - kj
    band = (rel >= 0) & (rel < WINDOW)
    pad_keys = (jnp.arange(n) == 0)[:, None, None] & (kj < ATT_BLOCK)[None]
    valid = band[None] & ~pad_keys
    scores = jnp.where(valid[None, :, None, None], scores, -jnp.inf)
    sink = sinks.astype(jnp.float32).reshape(ATT_KV_HEADS, ATT_GROUP)[None, None, :, :, None, None]
    m = jnp.maximum(jnp.max(scores, axis=-1, keepdims=True), sink)
    p = jnp.exp(scores - m)
    probs = p / (jnp.sum(p, axis=-1, keepdims=True) + jnp.exp(sink - m))
    out = jnp.einsum('bnhgqk,bnkhd->bnqhgd', probs.astype(v.dtype), vals)
    return out.reshape(B, T, ATT_WIDTH)


def setup_inputs(seed: int = 0) -> dict:
    key = jax.random.key(seed)
    ks = jax.random.split(key, 12)
    f32 = jnp.float32
    return {
        "x": jax.random.normal(ks[0], (BATCH, SEQ, D_MODEL), f32),
        "norm_w": 1.0 + 0.02 * jax.random.normal(ks[1], (DEPTH, D_MODEL), f32),
        "w_in": jax.random.normal(ks[2], (DEPTH, D_MODEL, D_IN), f32) * D_MODEL ** -0.5,
        "hgrn_lower_bound": 0.1 * jax.random.normal(ks[3], (DEPTH + 1, HG_KEY_DIM), f32),
        "hgrn_norm_w": 1.0 + 0.02 * jax.random.normal(ks[4], (DEPTH, HG_VAL_DIM), f32),
        "w_branch_hgrn": jax.random.normal(ks[5], (DEPTH, HG_VAL_DIM, D_MODEL), f32) * HG_VAL_DIM ** -0.5,
        "attn_sinks": 0.5 * jax.random.normal(ks[6], (DEPTH, ATT_Q_HEADS), f32),
        "w_branch_attn": jax.random.normal(ks[7], (DEPTH, ATT_WIDTH, D_MODEL), f32) * ATT_WIDTH ** -0.5,
        "w_out": jax.random.normal(ks[8], (DEPTH, D_MODEL, D_MODEL), f32) * D_MODEL ** -0.5,
        "final_norm_w": 1.0 + 0.02 * jax.random.normal(ks[9], (D_MODEL,), f32),
    }


def reference(x, norm_w, w_in, hgrn_lower_bound, hgrn_norm_w, w_branch_hgrn, attn_sinks,
              w_branch_attn, w_out, final_norm_w):
    B, T, _ = x.shape
    lb_all = jnp.cumsum(jax.nn.softmax(hgrn_lower_bound.astype(jnp.float32), axis=0), axis=0)
    for l in range(DEPTH):
        xn = rms_norm(x, norm_w[l])
        proj = xn @ w_in[l]
        hq, hf, hi, hg, aq, ak, av, ag, mh, ma = jnp.split(proj, SPLIT_POINTS, axis=-1)

        lb = lb_all[l]
        f = lb + (1.0 - lb) * jax.nn.sigmoid(hf.astype(jnp.float32))
        log_f = jnp.log(f).reshape(B, T, HG_HEADS, HG_EXPAND)
        k_h = (1.0 - f).reshape(B, T, HG_HEADS, HG_EXPAND)
        q_h = jax.nn.silu(hq.astype(jnp.float32)).reshape(B, T, HG_HEADS, HG_EXPAND)
        v_h = hi.astype(jnp.float32).reshape(B, T, HG_HEADS, HG_HEAD_V)
        o_h = hgrn2_chunkwise(q_h, k_h, v_h, log_f)
        o_h = rms_norm(o_h, hgrn_norm_w[l].reshape(HG_HEADS, HG_HEAD_V)).reshape(B, T, HG_VAL_DIM)
        y_h = (o_h.astype(x.dtype) * jax.nn.silu(hg)) @ w_branch_hgrn[l]

        o_a = sliding_window_attention_with_sinks(
            aq.reshape(B, T, ATT_Q_HEADS, ATT_HEAD_DIM),
            ak.reshape(B, T, ATT_KV_HEADS, ATT_HEAD_DIM),
            av.reshape(B, T, ATT_KV_HEADS, ATT_HEAD_DIM),
            attn_sinks[l])
        y_a = (o_a * jax.nn.silu(ag)) @ w_branch_attn[l]

        merged = jax.nn.sigmoid(mh) * y_h + jax.nn.sigmoid(ma) * y_a
        x = x + merged @ w_out[l]
    return rms_norm(x, final_norm_w)
```

```python
import contextlib
import numpy as np
import ml_dtypes
import concourse.bass as bass
import concourse.mybir as mybir
from concourse.bass_utils import run_bass_kernel_spmd

F32 = mybir.dt.float32
BF16 = mybir.dt.bfloat16
AF = mybir.ActivationFunctionType
ALU = mybir.AluOpType

P = 128
D = 1024
KC = 8
DIN = 8704
NCORES = 8
SEG_PER_BATCH = 4
OFF = dict(hq=0, hf=1024, hi=2048, hg=3072, aq=4096, ak=5120, av=5376, ag=5632, mh=6656, ma=7680)
EPS = 1e-6
NPREV = 3


class Sched:
    ENG = ['pe', 'act', 'dve', 'pool', 'sp']
    EPOCH = 1500

    def __init__(self, nc, es, n_dma_sems=12):
        self.nc = nc
        self.es = es
        self.streams = {e: [] for e in self.ENG}
        self.sem = {e: es.enter_context(nc.semaphore('c_' + e)) for e in self.ENG}
        self.cnt = {e: 0 for e in self.ENG}
        self.waited = {e: {} for e in self.ENG}
        self.res = {}
        self.dsem = [es.enter_context(nc.semaphore('d%d' % i)) for i in range(n_dma_sems)]
        self.dval = [0] * n_dma_sems
        self.dnext = 0
        self.semname = {}
        for e in self.ENG:
            self.semname[id(self.sem[e])] = self.sem[e]
        for s in self.dsem:
            self.semname[id(s)] = s
        self.all_events = {}
        self.skip_sids = set()
        self.nops = 0
        self.pending = []
        self.engobj = dict(pe=nc.tensor, act=nc.scalar, dve=nc.vector, pool=nc.gpsimd, sp=nc.sync)

    def _wait(self, eng, ev):
        sid, val = ev
        if self.waited[eng].get(sid, 0) >= val:
            return
        self.waited[eng][sid] = val
        sem = self.semname[sid]
        self.engobj[eng].wait_ge(sem, val)

    def _deps(self, eng, reads, writes):
        deps = []
        own = id(self.sem[eng])
        for r in reads:
            st = self.res.get(r)
            if st and st['w']:
                deps.append(st['w'])
        for w in writes:
            st = self.res.get(w)
            if st:
                if st['w']:
                    deps.append(st['w'])
                for sid, val in st['r'].items():
                    deps.append((sid, val))
        for ev in deps:
            if eng == 'pe' and ev[0] == own:
                continue
            self._wait(eng, ev)

    def _mark(self, ev, reads, writes):
        for r in reads:
            st = self.res.setdefault(r, {'w': None, 'r': {}})
            st['r'][ev[0]] = max(st['r'].get(ev[0], 0), ev[1])
        for w in writes:
            self.res[w] = {'w': ev, 'r': {}}
        self.all_events[ev[0]] = max(self.all_events.get(ev[0], 0), ev[1])

    COST = dict(pe=0.12, act=0.45, dve=0.40, pool=1.0, sp=1.5)
    LAT = 0.4
    SLACK = 0.15

    class _Probe:
        def __init__(self):
            self.out = None

        def __getattr__(self, name):
            def rec(*a, **k):
                o = k.get('out', a[0] if a else None)
                if o is None:
                    o = k.get('ap')
                self.out = o
                return self
            return rec

    def _est(self, eng, fn):
        try:
            pr = Sched._Probe()
            fn(pr)
            shp = tuple(pr.out.shape)
            n = 1
            for s_ in shp[1:]:
                n *= int(s_)
        except Exception:
            return self.COST[eng]
        if eng == 'pe':
            return 0.06 + n / 2400.0
        if eng == 'act':
            return 0.20 + n * 0.0006
        if eng == 'dve':
            return 0.20 + n * 0.0008
        if eng == 'pool':
            return 0.3 + n * 0.003
        return self.COST[eng]

    def op(self, eng, fn, reads=(), writes=(), cost=None):
        self.pending.append(('op', eng, fn, list(reads), list(writes), self._est(eng, fn)))

    def dma(self, out, in_, reads=(), writes=(), eng='sp'):
        try:
            nel = 1
            for s_ in tuple(out.shape):
                nel *= int(s_)
            cost = 2.0 + nel * 4 / 250e3
        except Exception:
            cost = self.COST['sp']
        self.pending.append(('dma', eng, (out, in_), list(reads), list(writes), cost))

    def flush(self):
        ops = self.pending
        self.pending = []
        n = len(ops)
        if n == 0:
            return
        lastw = {}
        readers = {}
        preds = [set() for _ in range(n)]
        for i, (kind, eng, fn, reads, writes, cost) in enumerate(ops):
            pk = [r for r in reads if isinstance(r, tuple) and r and r[0] == 'ps']
            rd = [r for r in reads if r not in pk]
            wr = list(writes) + [r for r in pk if r not in writes]
            for r in rd:
                if r in lastw:
                    preds[i].add(lastw[r])
            for w in wr:
                if w in lastw:
                    preds[i].add(lastw[w])
                for j in readers.get(w, ()):
                    preds[i].add(j)
            for r in rd:
                readers.setdefault(r, []).append(i)
            for w in wr:
                lastw[w] = i
                readers[w] = []
            preds[i].discard(i)
        succs = [[] for _ in range(n)]
        npred = [0] * n
        for i in range(n):
            npred[i] = len(preds[i])
            for j in preds[i]:
                succs[j].append(i)
        fin = [0.0] * n
        ready_t = [0.0] * n
        avail = {e: 0.0 for e in self.ENG}
        ready = [i for i in range(n) if npred[i] == 0]
        order = []
        last_eng_idx = {e: -1 for e in self.ENG}
        rank = [0.0] * n
        for i in range(n - 1, -1, -1):
            m_ = 0.0
            for s in succs[i]:
                if rank[s] > m_:
                    m_ = rank[s]
            rank[i] = ops[i][5] + self.LAT + m_
        while ready:
            sts = {}
            mn = None
            for i in ready:
                st = max(ready_t[i], avail[ops[i][1]])
                sts[i] = st
                if mn is None or st < mn:
                    mn = st
            best = None
            for i in ready:
                if sts[i] <= mn + self.SLACK:
                    key = (-rank[i], i)
                    if best is None or key < best[0]:
                        best = (key, i)
            i = best[1]
            st = sts[i]
            ready.remove(i)
            eng = ops[i][1]
            fin[i] = st + ops[i][5]
            avail[eng] = fin[i]
            order.append(i)
            for s in succs[i]:
                lat = 0.0 if (ops[s][1] == eng == 'pe') else self.LAT
                ready_t[s] = max(ready_t[s], fin[i] + lat)
                npred[s] -= 1
                if npred[s] == 0:
                    ready.append(s)
        assert len(order) == n
        for i in order:
            kind, eng, fn, reads, writes, cost = ops[i]
            if kind == 'op':
                self._op_now(eng, fn, reads, writes)
            else:
                self._dma_now(fn[0], fn[1], reads, writes, eng)

    def _op_now(self, eng, fn, reads=(), writes=()):
        self.nops += 1
        pk = [r for r in reads if isinstance(r, tuple) and r and r[0] == 'ps']
        if pk:
            reads = [r for r in reads if r not in pk]
            writes = list(writes) + [r for r in pk if r not in writes]
        self._deps(eng, reads, writes)
        if self.cnt[eng] >= self.EPOCH:
            ns = self.es.enter_context(self.nc.semaphore('c_%s_%d' % (eng, len(self.semname))))
            self.semname[id(ns)] = ns
            self.sem[eng] = ns
            self.cnt[eng] = 0
        self.cnt[eng] += 1
        sem = self.sem[eng]
        ev = (id(sem), self.cnt[eng])
        fn(self.engobj[eng]).then_inc(sem, 1)
        self._mark(ev, reads, writes)
        return ev

    def _dma_now(self, out, in_, reads=(), writes=(), eng='sp'):
        k = self.dnext
        self.dnext = (self.dnext + 1) % len(self.dsem)
        sem = self.dsem[k]
        if self.dval[k] > 0:
            self._wait(eng, (id(sem), self.dval[k]))
        self._deps(eng, reads, writes)
        self.dval[k] += 16
        ev = (id(sem), self.dval[k])
        self.engobj[eng].dma_start(out=out, in_=in_).then_inc(sem, 16)
        self._mark(ev, reads, writes)
        return ev

    def custom(self, eng, fn, sem, val, reads=(), writes=()):
        self._deps(eng, reads, writes)
        self.semname[id(sem)] = sem
        self.skip_sids.add(id(sem))
        ev = (id(sem), val)
        fn(self.engobj[eng])
        self._mark(ev, reads, writes)
        return ev

    def barrier(self):
        self.flush()
        for e in self.ENG:
            for sid, val in list(self.all_events.items()):
                if sid == id(self.sem[e]) or sid in self.skip_sids:
                    continue
                self._wait(e, (sid, val))
        keep = {k: {'w': v['w'], 'r': {}} for k, v in self.res.items() if v['w'] and v['w'][0] in self.skip_sids}
        self.res = keep

    def finish(self, eng='sp'):
        self.flush()
        for sid, val in list(self.all_events.items()):
            self._wait(eng, (sid, val))

    def emit(self):
        nc = self.nc
        with nc.Block() as block:
            @block.tensor
            def _(e):
                for f in self.streams['pe']:
                    f(e)

            @block.scalar
            def _(e):
                for f in self.streams['act']:
                    f(e)

            @block.vector
            def _(e):
                for f in self.streams['dve']:
                    f(e)

            @block.gpsimd
            def _(e):
                for f in self.streams['pool']:
                    f(e)

            @block.sync
            def _(e):
                for f in self.streams['sp']:
                    f(e)


def build_nc(NT, stop=None, use_cc=True, limit=None, cc_first=False, pad_kb=0):
    T = NT * P
    NB = NT // 4
    TA = (NT + 1) * P
    nc = bass.Bass("TRN2", target_bir_lowering=False)

    def din(name, shape, dt=F32):
        return nc.dram_tensor(name, list(shape), dt, kind="ExternalInput").ap()

    x_own = din("x_own", [T, D])
    x_halo = din("x_halo", [P, D])
    w_in = din("w_in", [D, DIN])
    w_bh = din("w_bh", [D, D])
    w_ba = din("w_ba", [D, D])
    w_out = din("w_out", [D, D])
    nw_d = din("nw", [P, KC])
    lbp_d = din("lbp", [P, 16])
    hnw_d = din("hnw", [P, 8])
    sink_d = din("sinks", [P, 16])
    fnw_d = din("fnw", [P, D])
    ident_d = din("ident", [P, P], BF16)
    mC_d = din("maskC", [P, 512], BF16)
    mP_d = din("maskP", [P, 512], BF16)
    mP0_d = din("maskP0", [P, 512], BF16)
    out_d = nc.dram_tensor("out", [T, D], F32, kind="ExternalOutput").ap()
    x_prev = din("x_prev", [NPREV * T, D])

    w_in_v = w_in.rearrange("(kc p) n -> p kc n", p=P)

    with contextlib.ExitStack() as es:
        S = Sched(nc, es)
        build_nc.last_sched = S

        uniq = [0]

        def sb(name, shape, dt=F32, stack=None):
            uniq[0] += 1
            return (stack or es).enter_context(nc.sbuf_tensor("sb%d_%s" % (uniq[0], name), list(shape), dt))

        if pad_kb:
            sb("padding", [P, pad_kb * 256], F32)
        ps = [es.enter_context(nc.psum_tensor("ps%d" % i, [P, 512], F32)) for i in range(8)]

        def PK(i):
            return ('ps', i)

        xnT = sb("xnT", [P, KC, TA], BF16)
        ident = sb("ident", [P, P], BF16)
        mC = sb("mC", [P, 512], BF16)
        mP = sb("mP", [P, 512], BF16)
        mP0 = sb("mP0", [P, 512], BF16)
        nw = sb("nw", [P, KC])
        lbp = sb("lbp", [P, 16])
        hnw = sb("hnw", [P, 8])
        esink = sb("esink", [P, 16])
        lb = sb("lb", [P, 8])
        ln1mlb = sb("ln1mlb", [P, 8])
        tmp8 = sb("tmp8", [P, 8])
        ones = sb("ones", [P, P])
        epsb = sb("epsb", [P, 1])
        S_in = sb("S_in", [P, 8, P])

        def xk(ti):
            return [('xnT', ti, c) for c in range(KC)]

        def xkb(t0, t1):
            r = []
            for ti in range(t0, t1):
                r += xk(ti)
            return r

        for (dst, src, key) in [(ident, ident_d, 'ident'), (mC, mC_d, 'mC'), (mP, mP_d, 'mP'), (mP0, mP0_d, 'mP0'),
                                (nw, nw_d, 'nw'), (lbp, lbp_d, 'lbp'), (hnw, hnw_d, 'hnw'), (esink, sink_d, 'esink')]:
            S.dma(dst[:], src, writes=[key])
        S.op('dve', lambda e: e.memset(ones[:], 1.0), writes=['ones'])
        S.op('dve', lambda e: e.memset(epsb[:], EPS), writes=['epsb'])
        S.op('dve', lambda e: e.memset(S_in[:], 0.0), writes=[('S_in', h_) for h_ in range(8)])
        S.op('dve', lambda e: e.tensor_tensor(out=tmp8[:], in0=lbp[:, 8:16], in1=lbp[:, 0:8], op=ALU.subtract),
             reads=['lbp'], writes=['tmp8'])
        S.op('act', lambda e: e.activation(out=tmp8[:], in_=tmp8[:], func=AF.Exp), reads=['tmp8'], writes=['tmp8'])
        S.op('act', lambda e: e.activation(out=tmp8[:], in_=tmp8[:], func=AF.Ln, bias=1.0), reads=['tmp8'], writes=['tmp8'])
        S.op('act', lambda e: e.activation(out=lb[:], in_=tmp8[:], func=AF.Exp, scale=-1.0), reads=['tmp8'], writes=['lb'])
        S.op('act', lambda e: e.activation(out=ln1mlb[:], in_=lb[:], func=AF.Ln, scale=-1.0, bias=1.0),
             reads=['lb'], writes=['ln1mlb'])
        S.op('act', lambda e: e.activation(out=esink[:], in_=esink[:], func=AF.Exp), reads=['esink'], writes=['esink'])

        lw_cnt = [0]

        def load_w(stg, dst_ap, src_ap, key, skey):
            S.dma(stg, src_ap, writes=[skey])
            lw_cnt[0] += 1
            if lw_cnt[0] % 3 == 0:
                S.op('act', lambda e: e.activation(out=dst_ap, in_=stg, func=AF.Copy), reads=[skey], writes=[key], cost=1.5)
            else:
                S.op('pool', lambda e: e.tensor_copy(out=dst_ap, in_=stg), reads=[skey], writes=[key], cost=2.5)

        def stage_a_alloc(ph):
            return dict(xt=[sb("xt%d" % i, [P, D], F32, ph) for i in range(3)],
                        xs=[sb("xs%d" % i, [P, D], BF16, ph) for i in range(2)],
                        junk=sb("junkA", [P, D], BF16, ph),
                        ssq=sb("ssq", [P, NT + 1], F32, ph),
                        lnms=sb("lnms", [P, NT + 1], F32, ph),
                        rstd=sb("rstd", [P, NT + 1], F32, ph))

        def stage_a(tiles, ctx=None):
            with contextlib.ExitStack() as ph:
                A = ctx if ctx is not None else stage_a_alloc(ph)
                xt, xs, junk, ssq, lnms, rstd = A['xt'], A['xs'], A['junk'], A['ssq'], A['lnms'], A['rstd']
                S.op('dve', lambda e: e.memset(ssq[:], 0.0), writes=[('ssq', i) for i in range(NT + 1)])
                for n_, (ti, src_ap) in enumerate(tiles):
                    b = n_ % 2
                    b3 = n_ % 3
                    S.dma(xt[b3][:], src_ap, writes=[('xt', b3)])
                    S.op('act', lambda e, b3=b3, ti=ti: e.activation(out=junk[:], in_=xt[b3][:], func=AF.Square,
                                                                accum_out=ssq[:, ti:ti + 1]),
                         reads=[('xt', b3), ('ssq', ti)], writes=['junkA', ('ssq', ti)], cost=1.1)
                    S.op('act', lambda e, ti=ti: e.activation(out=lnms[:, ti:ti + 1], in_=ssq[:, ti:ti + 1], func=AF.Ln,
                                                          scale=1.0 / D, bias=epsb[:]),
                         reads=[('ssq', ti), 'epsb'], writes=[('lnms', ti)])
                    S.op('act', lambda e, ti=ti: e.activation(out=rstd[:, ti:ti + 1], in_=lnms[:, ti:ti + 1], func=AF.Exp,
                                                          scale=-0.5),
                         reads=[('lnms', ti)], writes=[('rstd', ti)])
                    S.op('dve', lambda e, b=b, b3=b3, ti=ti: e.tensor_scalar(out=xs[b][:], in0=xt[b3][:], scalar1=rstd[:, ti:ti + 1],
                                                                  scalar2=None, op0=ALU.mult),
                         reads=[('xt', b3), ('rstd', ti)], writes=[('xs', b)], cost=0.8)
                    pst = ps[b].bitcast(BF16).rearrange("p (c t) -> p c t", c=KC)
                    for c in range(KC):
                        S.op('pe', lambda e, b=b, c=c, pst=pst: e.transpose(out=pst[:, c, :], in_=xs[b][:, c * P:(c + 1) * P],
                                                                        identity=ident[:]),
                             reads=[('xs', b), 'ident'], writes=[PK(b)])
                    for c in range(KC):
                        dst = xnT[:, c, ti * P:(ti + 1) * P]
                        if c % 2 == 0:
                            S.op('dve', lambda e, c=c, pst=pst, dst=dst: e.tensor_scalar(out=dst, in0=pst[:, c, :],
                                                                                    scalar1=nw[:, c:c + 1], scalar2=None,
                                                                                    op0=ALU.mult),
                                 reads=[PK(b), 'nw'], writes=[('xnT', ti, c)])
                        else:
                            S.op('act', lambda e, c=c, pst=pst, dst=dst: e.activation(out=dst, in_=pst[:, c, :], func=AF.Copy,
                                                                                 scale=nw[:, c:c + 1]),
                                 reads=[PK(b), 'nw'], writes=[('xnT', ti, c)])
            if ctx is None:
                S.barrier()

        def hgrn_phase(full):
            with contextlib.ExitStack() as ph:
                tag = 'F' if full else 'S'
                stg = sb("stg" + tag, [P, KC, 512], F32, ph)
                Wh = [sb("Wh%s%d" % (tag, i), [P, KC, 512], BF16, ph) for i in range(4)]
                e_tL = [sb("e_tF%d" % i_, [P, 512], F32, ph) for i_ in range(2)]
                L1L = [sb("L1F%d" % i_, [P, 512], F32, ph) for i_ in range(2)]
                g_tL = [sb("g_tF%d" % i_, [P, 512], F32, ph) for i_ in range(2)]
                b_tL = [sb("b_tF%d" % i_, [P, 512], F32, ph) for i_ in range(2)]
                t_tL = [sb("t_tF%d" % i_, [P, 512], F32, ph) for i_ in range(2)]
                ktTL = [sb("ktTF%d" % i_, [P, 512], BF16, ph) for i_ in range(2)]
                vbfL = [sb("vbfF%d" % i_, [P, 4, P], BF16, ph) for i_ in range(2)]
                ktokL = [sb("ktokH%d" % i_, [P, P], BF16, ph) for i_ in range(2)]
                S32L = [sb("S32H%d" % i_, [P, P], F32, ph) for i_ in range(2)]
                StmpL = [sb("StmpH%d" % i_, [P, P], F32, ph) for i_ in range(2)]
                decL = [sb("decH%d" % i_, [P, 1], F32, ph) for i_ in range(2)]
                sumb = sb("sumb" + tag, [P, 8], F32, ph)
                if full:
                    eqL = [sb("eqF%d" % i_, [P, 512], F32, ph) for i_ in range(2)]
                    qtTL = [sb("qtTF%d" % i_, [P, 512], BF16, ph) for i_ in range(2)]
                    gateL = [sb("gateF%d" % i_, [P, 4, P], F32, ph) for i_ in range(2)]
                    SbfL = [sb("SbfH%d" % i_, [P, P], BF16, ph) for i_ in range(2)]
                    ATmL = [sb("ATmH%d" % i_, [P, P], BF16, ph) for i_ in range(2)]
                    junkL = [sb("junkHH%d" % i_, [P, P], BF16, ph) for i_ in range(2)]
                    ssoL = [sb("ssoH%d" % i_, [P, 1], F32, ph) for i_ in range(2)]
                    rsoL = [sb("rsoH%d" % i_, [P, 1], F32, ph) for i_ in range(2)]
                    utokL = [sb("utokH%d" % i_, [P, P], BF16, ph) for i_ in range(2)]

                def load_head(h):
                    wb = Wh[h % 4]
                    for gi, nm in enumerate(['hq', 'hf', 'hi', 'hg']):
                        if not full and nm in ('hq', 'hg'):
                            continue
                        c0 = OFF[nm] + h * P
                        load_w(stg[:, :, gi * P:(gi + 1) * P], wb[:, :, gi * P:(gi + 1) * P], w_in_v[:, :, c0:c0 + P],
                               ('Wh', h % 4, gi), ('stg', gi))

                def do_block(h, j, fb, wb, wk):
                    hp = h % 2
                    ktok = ktokL[hp]
                    S32 = S32L[hp]
                    Stmp = StmpL[hp]
                    dec = decL[hp]
                    Sbf = SbfL[hp]
                    ATm = ATmL[hp]
                    junk = junkL[hp]
                    sso = ssoL[hp]
                    rso = rsoL[hp]
                    utok = utokL[hp]
                    e_t = e_tL[fb]
                    L1 = L1L[fb]
                    g_t = g_tL[fb]
                    b_t = b_tL[fb]
                    t_t = t_tL[fb]
                    ktT = ktTL[fb]
                    vbf = vbfL[fb]
                    eq = eqL[fb]
                    qtT = qtTL[fb]
                    gate = gateL[fb]
                    tok0 = P + j * 512
                    rk = xkb(1 + 4 * j, 5 + 4 * j)
                    for kc in range(KC):
                        S.op('pe', lambda e, kc=kc, wb=wb, tok0=tok0: e.matmul(ps[0][:], lhsT=wb[:, kc, P:2 * P],
                                                                            rhs=xnT[:, kc, tok0:tok0 + 512],
                                                                            start=(kc == 0), stop=(kc == KC - 1)),
                             reads=rk + [wk[1]], writes=[PK(0)])
                    if full:
                        for kc in range(KC):
                            S.op('pe', lambda e, kc=kc, wb=wb, tok0=tok0: e.matmul(ps[1][:], lhsT=wb[:, kc, 0:P],
                                                                                rhs=xnT[:, kc, tok0:tok0 + 512],
                                                                                start=(kc == 0), stop=(kc == KC - 1)),
                                 reads=rk + [wk[0]], writes=[PK(1)])
                    for tl in range(4):
                        c0 = tok0 + tl * P
                        bank = 2 + tl // 2
                        half = tl % 2
                        if full:
                            o_ap = ps[bank][:, half * 256:(half + 1) * 256]
                            r_sl = slice(2 * P, 4 * P)
                            rkk = [wk[2], wk[3]]
                        else:
                            o_ap = ps[bank][:, half * 256:half * 256 + P]
                            r_sl = slice(2 * P, 3 * P)
                            rkk = [wk[2]]
                        for kc in range(KC):
                            S.op('pe', lambda e, kc=kc, wb=wb, c0=c0, o_ap=o_ap, r_sl=r_sl: e.matmul(
                                o_ap, lhsT=xnT[:, kc, c0:c0 + P], rhs=wb[:, kc, r_sl],
                                start=(kc == 0), stop=(kc == KC - 1)),
                                 reads=xk(1 + 4 * j + tl) + rkk, writes=[PK(bank)])
                    S.op('act', lambda e: e.activation(out=e_t[:], in_=ps[0][:], func=AF.Exp, scale=-1.0),
                         reads=[PK(0)], writes=[('e_t', fb)])
                    S.op('act', lambda e: e.activation(out=L1[:], in_=e_t[:], func=AF.Ln, bias=1.0),
                         reads=[('e_t', fb)], writes=[('L1', fb)])
                    S.op('act', lambda e, h=h: e.activation(out=g_t[:], in_=e_t[:], func=AF.Ln, scale=lb[:, h:h + 1],
                                                        bias=1.0),
                         reads=[('e_t', fb), 'lb'], writes=[('g_t', fb)])
                    S.op('dve', lambda e: e.tensor_tensor(out=g_t[:], in0=g_t[:], in1=L1[:], op=ALU.subtract),
                         reads=[('g_t', fb), ('L1', fb)], writes=[('g_t', fb)])
                    for tl in range(4):
                        S.op('dve', lambda e, tl=tl: e.tensor_tensor_scan(out=b_t[:, tl * P:(tl + 1) * P], data0=ones[:],
                                                                     data1=g_t[:, tl * P:(tl + 1) * P], initial=0.0,
                                                                     op0=ALU.mult, op1=ALU.add),
                             reads=[('g_t', fb), 'ones'], writes=[('b_t', fb, tl)])
                    bk = [('b_t', fb, tl) for tl in range(4)]
                    S.op('dve', lambda e: e.scalar_tensor_tensor(out=t_t[:], in0=ps[0][:], scalar=-1.0, in1=L1[:],
                                                              op0=ALU.mult, op1=ALU.subtract),
                         reads=[PK(0), ('L1', fb)], writes=[('t_t', fb)])
                    S.op('dve', lambda e: e.tensor_tensor(out=t_t[:], in0=t_t[:], in1=b_t[:], op=ALU.subtract),
                         reads=[('t_t', fb)] + bk, writes=[('t_t', fb)])
                    S.op('act', lambda e, h=h: e.activation(out=ktT[:], in_=t_t[:], func=AF.Exp,
                                                        bias=ln1mlb[:, h:h + 1]),
                         reads=[('t_t', fb), 'ln1mlb'], writes=[('ktT', fb)])
                    if full:
                        S.op('act', lambda e: e.activation(out=eq[:], in_=ps[1][:], func=AF.Exp, scale=-1.0),
                             reads=[PK(1)], writes=[('eq', fb)])
                        S.op('act', lambda e: e.activation(out=eq[:], in_=eq[:], func=AF.Ln, bias=1.0),
                             reads=[('eq', fb)], writes=[('eq', fb)])
                        S.op('dve', lambda e: e.tensor_tensor(out=eq[:], in0=b_t[:], in1=eq[:], op=ALU.subtract),
                             reads=[('eq', fb)] + bk, writes=[('eq', fb)])
                        S.op('act', lambda e: e.activation(out=eq[:], in_=eq[:], func=AF.Exp), reads=[('eq', fb)], writes=[('eq', fb)])
                        S.op('dve', lambda e: e.tensor_tensor(out=qtT[:], in0=ps[1][:], in1=eq[:], op=ALU.mult),
                             reads=[PK(1), ('eq', fb)], writes=[('qtT', fb)])
                        for bank in (2, 3):
                            gsrc = ps[bank].rearrange("p (t c) -> p t c", t=2)[:, :, P:2 * P]
                            gdst = gate[:, (bank - 2) * 2:(bank - 2) * 2 + 2, :]
                            S.op('act', lambda e, gsrc=gsrc, gdst=gdst: e.activation(out=gdst, in_=gsrc, func=AF.Exp, scale=-1.0),
                                 reads=[PK(bank)], writes=[('gate', fb, bank)])
                            S.op('act', lambda e, gdst=gdst: e.activation(out=gdst, in_=gdst, func=AF.Ln, bias=1.0),
                                 reads=[('gate', fb, bank)], writes=[('gate', fb, bank)])
                            S.op('act', lambda e, gdst=gdst: e.activation(out=gdst, in_=gdst, func=AF.Exp, scale=-1.0),
                                 reads=[('gate', fb, bank)], writes=[('gate', fb, bank)])
                            S.op('dve', lambda e, gsrc=gsrc, gdst=gdst: e.tensor_tensor(out=gdst, in0=gsrc, in1=gdst, op=ALU.mult),
                                 reads=[PK(bank), ('gate', fb, bank)], writes=[('gate', fb, bank)])
                    for bank in (2, 3):
                        vsrc = ps[bank].rearrange("p (t c) -> p t c", t=2)[:, :, 0:P]
                        vdst = vbf[:, (bank - 2) * 2:(bank - 2) * 2 + 2, :]
                        S.op('act', lambda e, vsrc=vsrc, vdst=vdst: e.activation(out=vdst, in_=vsrc, func=AF.Copy),
                             reads=[PK(bank)], writes=[('vbf', fb, bank)])
                    for tl in range(4):
                        ti = 4 * j + tl
                        sl = slice(tl * P, (tl + 1) * P)
                        vk = ('vbf', fb, 2 + tl // 2)
                        v_ap = vbf[:, tl, :]
                        S.op('act', lambda e, tl=tl: e.activation(out=dec[:], in_=b_t[:, tl * P + P - 1:tl * P + P], func=AF.Exp),
                             reads=[('b_t', fb, tl)], writes=[('dec', hp)])
                        kt_ps = ps[6].bitcast(BF16)[:, 0:P]
                        S.op('pe', lambda e, sl=sl, kt_ps=kt_ps: e.transpose(out=kt_ps, in_=ktT[:, sl], identity=ident[:]),
                             reads=[('ktT', fb), 'ident'], writes=[PK(6)])
                        S.op('dve', lambda e, kt_ps=kt_ps: e.tensor_copy(out=ktok[:], in_=kt_ps), reads=[PK(6)], writes=[('ktok', hp)])
                        if full:
                            S.op('pe', lambda e, sl=sl: e.matmul(ps[4][:, 0:P], lhsT=qtT[:, sl], rhs=Sbf[:], start=True, stop=False),
                                 reads=[('qtT', fb), ('Sbf', hp)], writes=[PK(4)])
                            S.op('pe', lambda e, sl=sl: e.matmul(ps[5][:, 0:P], lhsT=ktT[:, sl], rhs=qtT[:, sl], start=True, stop=True),
                                 reads=[('qtT', fb), ('ktT', fb)], writes=[PK(5)])
                            S.op('dve', lambda e: e.tensor_tensor(out=ATm[:], in0=ps[5][:, 0:P], in1=mC[:, 0:P], op=ALU.mult),
                                 reads=[PK(5), 'mC'], writes=[('ATm', hp)])
                            S.op('pe', lambda e, v_ap=v_ap: e.matmul(ps[4][:, 0:P], lhsT=ATm[:], rhs=v_ap, start=False, stop=True),
                                 reads=[('ATm', hp), vk], writes=[PK(4)])
                        S.op('pe', lambda e, v_ap=v_ap: e.matmul(ps[7][:, 0:P], lhsT=ktok[:], rhs=v_ap, start=True, stop=True),
                             reads=[('ktok', hp), vk], writes=[PK(7)])
                        S.op('dve', lambda e: e.tensor_scalar(out=Stmp[:], in0=S32[:], scalar1=dec[:, 0:1], scalar2=None, op0=ALU.mult),
                             reads=[('S32', hp), ('dec', hp)], writes=[('Stmp', hp)])
                        S.op('dve', lambda e: e.scalar_tensor_tensor(out=S32[:], in0=ps[7][:, 0:P], scalar=dec[:, 0:1], in1=Stmp[:],
                                                                  op0=ALU.mult, op1=ALU.add),
                             reads=[PK(7), ('dec', hp), ('Stmp', hp)], writes=[('S32', hp)])
                        if full:
                            S.op('act', lambda e: e.activation(out=junk[:], in_=ps[4][:, 0:P], func=AF.Square, accum_out=sso[:]),
                                 reads=[PK(4)], writes=[('junkH', hp), ('sso', hp)])
                            S.op('act', lambda e: e.activation(out=rso[:], in_=sso[:], func=AF.Ln, scale=1.0 / P, bias=epsb[:]),
                                 reads=[('sso', hp), 'epsb'], writes=[('rso', hp)])
                            S.op('act', lambda e: e.activation(out=rso[:], in_=rso[:], func=AF.Exp, scale=-0.5),
                                 reads=[('rso', hp)], writes=[('rso', hp)])
                            S.op('dve', lambda e, tl=tl: e.scalar_tensor_tensor(out=utok[:], in0=ps[4][:, 0:P], scalar=rso[:, 0:1],
                                                                            in1=gate[:, tl, :], op0=ALU.mult, op1=ALU.mult),
                                 reads=[PK(4), ('rso', hp), ('gate', fb, 2 + tl // 2)], writes=[('utok', hp)])
                            S.op('dve', lambda e: e.tensor_copy(out=Sbf[:], in_=S32[:]), reads=[('S32', hp)], writes=[('Sbf', hp)])
                            ut_ps = ps[6].bitcast(BF16)[:, 0:P]
                            S.op('pe', lambda e, ut_ps=ut_ps: e.transpose(out=ut_ps, in_=utok[:], identity=ident[:]),
                                 reads=[('utok', hp), 'ident'], writes=[PK(6)])
                            S.op('act', lambda e, ut_ps=ut_ps, h=h, ti=ti: e.activation(out=uhT[:, h, ti * P:(ti + 1) * P], in_=ut_ps,
                                                                                   func=AF.Copy, scale=hnw[:, h:h + 1]),
                                 reads=[PK(6), 'hnw'], writes=[('uhT', h, ti)])

                load_head(0)
                load_head(1)
                cnt_ = [0]
                for pr in range(4):
                    for hh_ in (2 * pr + 2, 2 * pr + 3):
                        if hh_ < 8:
                            load_head(hh_)
                    for h in (2 * pr, 2 * pr + 1):
                        hp = h % 2
                        S.op('dve', lambda e, h=h, hp=hp: e.tensor_copy(out=S32L[hp][:], in_=S_in[:, h, :]), reads=[('S_in', h)], writes=[('S32', hp)])
                        S.op('dve', lambda e, h=h, hp=hp: e.tensor_copy(out=SbfL[hp][:], in_=S_in[:, h, :]), reads=[('S_in', h)], writes=[('Sbf', hp)])
                    for j in range(NB):
                        for h in (2 * pr, 2 * pr + 1):
                            do_block(h, j, cnt_[0] % 2, Wh[h % 4], [('Wh', h % 4, gi) for gi in range(4)])
                            cnt_[0] += 1
            S.barrier()

        def summary_alloc(ph):
            return dict(
                e_t=[sb("pe_t%d" % i, [P, 512], F32, ph) for i in range(4)],
                L1=[sb("pL1%d" % i, [P, 512], F32, ph) for i in range(4)],
                g_t=[sb("pg_t%d" % i, [P, 512], F32, ph) for i in range(4)],
                b_t=[sb("pb_t%d" % i, [P, 512], F32, ph) for i in range(4)],
                t_t=[sb("pt_t%d" % i, [P, 512], F32, ph) for i in range(4)],
                ktT=[sb("pktT%d" % i, [P, 512], BF16, ph) for i in range(6)],
                decs=[sb("pdecs%d" % i, [P, 4], F32, ph) for i in range(6)],
                vall=[sb("pvall%d" % i, [P, 4, D], BF16, ph) for i in range(2)],
                ktok=[sb("pktok%d" % i, [P, P], BF16, ph) for i in range(2)],
                tmpS=[sb("ptmpS%d" % i, [P, P], F32, ph) for i in range(2)])

        def summary_pass(Wf, Wi, ctx=None):
            with contextlib.ExitStack() as ph:
                C = ctx if ctx is not None else summary_alloc(ph)
                e_t, L1, g_t, b_t, t_t, ktT, decs, vall, ktok, tmpS = (C[k_] for k_ in
                    ('e_t', 'L1', 'g_t', 'b_t', 't_t', 'ktT', 'decs', 'vall', 'ktok', 'tmpS'))

                def vproj(j):
                    jb = j % 2
                    tok0 = P + j * 512
                    for tl in range(4):
                        c0 = tok0 + tl * P
                        for half in range(2):
                            bank = 2 + (2 * tl + half) % 2
                            for kc in range(KC):
                                S.op('pe', lambda e, kc=kc, c0=c0, half=half, bank=bank: e.matmul(
                                    ps[bank][:], lhsT=xnT[:, kc, c0:c0 + P], rhs=Wi[:, kc, half * 512:(half + 1) * 512],
                                    start=(kc == 0), stop=(kc == KC - 1)),
                                     reads=xk(1 + 4 * j + tl) + [('Wi', half)], writes=[PK(bank)], cost=0.22)
                            S.op('act', lambda e, jb=jb, tl=tl, half=half, bank=bank: e.activation(
                                out=vall[jb][:, tl, half * 512:(half + 1) * 512], in_=ps[bank][:], func=AF.Copy),
                                 reads=[PK(bank)], writes=[('vall', jb, tl, half)])

                def front(idx, j, h):
                    pb = idx % 2
                    bf = idx % 4
                    hb = idx % 6
                    tok0 = P + j * 512
                    rk = xkb(1 + 4 * j, 5 + 4 * j)
                    for kc in range(KC):
                        S.op('pe', lambda e, kc=kc, tok0=tok0, h=h, pb=pb: e.matmul(
                            ps[pb][:], lhsT=Wf[:, kc, h * P:(h + 1) * P], rhs=xnT[:, kc, tok0:tok0 + 512],
                            start=(kc == 0), stop=(kc == KC - 1)),
                             reads=rk + [('Wf', h // 4)], writes=[PK(pb)], cost=0.22)
                    S.op('act', lambda e, bf=bf, pb=pb: e.activation(out=e_t[bf][:], in_=ps[pb][:], func=AF.Exp, scale=-1.0),
                         reads=[PK(pb)], writes=[('e_t', bf)])
                    S.op('dve', lambda e, bf=bf, pb=pb: e.tensor_scalar(out=t_t[bf][:], in0=ps[pb][:], scalar1=-1.0, scalar2=None, op0=ALU.mult),
                         reads=[PK(pb)], writes=[('t_t', bf)])
                    S.op('act', lambda e, bf=bf: e.activation(out=L1[bf][:], in_=e_t[bf][:], func=AF.Ln, bias=1.0),
                         reads=[('e_t', bf)], writes=[('L1', bf)])
                    S.op('act', lambda e, bf=bf, h=h: e.activation(out=g_t[bf][:], in_=e_t[bf][:], func=AF.Ln, scale=lb[:, h:h + 1], bias=1.0),
                         reads=[('e_t', bf), 'lb'], writes=[('g_t', bf)])
                    S.op('dve', lambda e, bf=bf: e.tensor_tensor(out=g_t[bf][:], in0=g_t[bf][:], in1=L1[bf][:], op=ALU.subtract),
                         reads=[('g_t', bf), ('L1', bf)], writes=[('g_t', bf)])
                    for tl in range(4):
                        S.op('dve', lambda e, tl=tl, bf=bf: e.tensor_tensor_scan(out=b_t[bf][:, tl * P:(tl + 1) * P], data0=ones[:],
                                                                            data1=g_t[bf][:, tl * P:(tl + 1) * P], initial=0.0,
                                                                            op0=ALU.mult, op1=ALU.add),
                             reads=[('g_t', bf), 'ones'], writes=[('b_t', bf, tl)])
                    bk = [('b_t', bf, tl) for tl in range(4)]
                    S.op('dve', lambda e, bf=bf: e.tensor_tensor(out=t_t[bf][:], in0=t_t[bf][:], in1=L1[bf][:], op=ALU.subtract),
                         reads=[('t_t', bf), ('L1', bf)], writes=[('t_t', bf)])
                    S.op('dve', lambda e, bf=bf: e.tensor_tensor(out=t_t[bf][:], in0=t_t[bf][:], in1=b_t[bf][:], op=ALU.subtract),
                         reads=[('t_t', bf)] + bk, writes=[('t_t', bf)])
                    S.op('act', lambda e, bf=bf, hb=hb, h=h: e.activation(out=ktT[hb][:], in_=t_t[bf][:], func=AF.Exp, bias=ln1mlb[:, h:h + 1]),
                         reads=[('t_t', bf), 'ln1mlb'], writes=[('ktT', hb)])
                    blast = b_t[bf].rearrange("p (t c) -> p t c", t=4)[:, :, P - 1]
                    S.op('act', lambda e, bf=bf, hb=hb, blast=blast: e.activation(out=decs[hb][:], in_=blast, func=AF.Exp),
                         reads=bk, writes=[('decs', hb)])

                def back(idx, j, h):
                    bf = idx % 2
                    hb = idx % 6
                    jb = j % 2
                    for tl in range(4):
                        tb = tl % 2
                        kt_ps = ps[4 + tb].bitcast(BF16)[:, 0:P]
                        S.op('pe', lambda e, tl=tl, hb=hb, kt_ps=kt_ps: e.transpose(out=kt_ps, in_=ktT[hb][:, tl * P:(tl + 1) * P], identity=ident[:]),
                             reads=[('ktT', hb), 'ident'], writes=[PK(4 + tb)])
                        if tb == 0:
                            S.op('dve', lambda e, tb=tb, kt_ps=kt_ps: e.tensor_copy(out=ktok[tb][:], in_=kt_ps), reads=[PK(4 + tb)], writes=[('ktok', tb)])
                        else:
                            S.op('act', lambda e, tb=tb, kt_ps=kt_ps: e.activation(out=ktok[tb][:], in_=kt_ps, func=AF.Copy), reads=[PK(4 + tb)], writes=[('ktok', tb)])
                        S.op('pe', lambda e, tl=tl, tb=tb, jb=jb, h=h: e.matmul(ps[6 + tb][:, 0:P], lhsT=ktok[tb][:], rhs=vall[jb][:, tl, h * P:(h + 1) * P],
                                                                          start=True, stop=True),
                             reads=[('ktok', tb), ('vall', jb, tl, h // 4)], writes=[PK(6 + tb)])
                        S.op('act', lambda e, tl=tl, tb=tb, hb=hb: e.activation(out=tmpS[tb][:], in_=ps[6 + tb][:, 0:P], func=AF.Copy,
                                                                          scale=decs[hb][:, tl:tl + 1]),
                             reads=[PK(6 + tb), ('decs', hb)], writes=[('tmpS', tb)])
                        S.op('dve', lambda e, tl=tl, tb=tb, hb=hb, h=h: e.scalar_tensor_tensor(out=S_in[:, h, :], in0=S_in[:, h, :], scalar=decs[hb][:, tl:tl + 1],
                                                                                     in1=tmpS[tb][:], op0=ALU.mult, op1=ALU.add),
                             reads=[('tmpS', tb), ('decs', hb), ('S_in', h)], writes=[('S_in', h)])

                seq = [(j, h) for j in range(NB) for h in range(8)]
                vproj(0)
                front(0, *seq[0])
                for idx, (j, h) in enumerate(seq):
                    if idx + 1 < len(seq):
                        nj, nh = seq[idx + 1]
                        if nh == 0:
                            vproj(nj)
                        front(idx + 1, nj, nh)
                    back(idx, j, h)
            if ctx is None:
                S.barrier()

        with contextlib.ExitStack() as pre:
            stg1 = sb("stg1", [P, KC, 512], F32, pre)
            Wf = sb("Wf", [P, KC, D], BF16, pre)
            Wi = sb("Wi", [P, KC, D], BF16, pre)
            for hh in range(2):
                load_w(stg1[:], Wf[:, :, hh * 512:(hh + 1) * 512], w_in_v[:, :, OFF['hf'] + hh * 512:OFF['hf'] + (hh + 1) * 512],
                       ('Wf', hh), 'stg1')
                load_w(stg1[:], Wi[:, :, hh * 512:(hh + 1) * 512], w_in_v[:, :, OFF['hi'] + hh * 512:OFF['hi'] + (hh + 1) * 512],
                       ('Wi', hh), 'stg1')
            actx = stage_a_alloc(pre)
            sctx = summary_alloc(pre)
            for k in range(NPREV):
                stage_a([(1 + i, x_prev[(k * NT + i) * P:(k * NT + i + 1) * P, :]) for i in range(NT)], actx)
                summary_pass(Wf, Wi, sctx)
            stage_a([(0, x_halo)] + [(1 + i, x_own[i * P:(i + 1) * P, :]) for i in range(NT)], actx)
        S.barrier()
        uhT = sb("uhT", [P, KC, T], BF16)
        uaT = sb("uaT", [P, KC, T], BF16)

        with contextlib.ExitStack() as ph:
            stg = sb("stgA", [P, KC, 704], F32, ph)
            Wa = [sb("Wa%d" % i, [P, KC, 704], BF16, ph) for i in range(2)]
            kT = sb("kT", [P, TA], BF16, ph)
            qT = sb("qT", [P, 2, 2, T], BF16, ph)
            S.op('pool', lambda e: e.memset(qT[:], 0.0), writes=['qTz'])
            vext = sb("vext", [P, NT + 1, 65], BF16, ph)
            ga = sb("ga", [P, NT, 256], BF16, ph)
            gtmp = sb("gtmp", [P, 256], F32, ph)
            Pm = [sb("Pm%d" % i, [P, 512], BF16, ph) for i in range(2)]
            den = sb("den", [P, 4], F32, ph)
            uat = sb("uat", [P, 256], BF16, ph)

            def load_grp(g):
                wb = Wa[g % 2]
                parts = [(0, 256, OFF['aq'] + g * 256, 0), (256, 64, OFF['ak'] + g * 64, 1), (320, 64, OFF['ak'] + g * 64, 2),
                         (384, 64, OFF['av'] + g * 64, 3), (448, 256, OFF['ag'] + g * 256, 4)]
                for (d0, n, c0, pi) in parts:
                    load_w(stg[:, :, d0:d0 + n], wb[:, :, d0:d0 + n], w_in_v[:, :, c0:c0 + n], ('Wa', g % 2, pi), ('stgA', pi))

            load_grp(0)
            for g in range(4):
                if g + 1 < 4:
                    load_grp(g + 1)
                wb = Wa[g % 2]
                wk = [('Wa', g % 2, pi) for pi in range(5)]
                S.op('dve', lambda e: e.memset(vext[:, :, 64:65], 1.0), writes=[('vext1',)])
                c0 = 0
                while c0 < TA:
                    n = min(512, TA - c0)
                    for kc in range(KC):
                        S.op('pe', lambda e, kc=kc, c0=c0, n=n, wb=wb: e.matmul(ps[0][:, 0:n], lhsT=wb[:, kc, 256:384],
                                                                             rhs=xnT[:, kc, c0:c0 + n], start=(kc == 0), stop=(kc == KC - 1)),
                             reads=xkb(c0 // P, (c0 + n) // P) + [wk[1], wk[2]], writes=[PK(0)])
                    S.op('act', lambda e, c0=c0, n=n: e.activation(out=kT[:, c0:c0 + n], in_=ps[0][:, 0:n], func=AF.Copy),
                         reads=[PK(0)], writes=[('kT', c0 // 512)])
                    c0 += n
                for j in range(NB):
                    tok0 = P + j * 512
                    for ch in range(2):
                        for kc in range(KC):
                            S.op('pe', lambda e, kc=kc, ch=ch, tok0=tok0, wb=wb: e.matmul(ps[1][:], lhsT=wb[:, kc, ch * P:(ch + 1) * P],
                                                                                       rhs=xnT[:, kc, tok0:tok0 + 512],
                                                                                       start=(kc == 0), stop=(kc == KC - 1)),
                                 reads=xkb(1 + 4 * j, 5 + 4 * j) + [wk[0]], writes=[PK(1)])
                        S.op('dve', lambda e, ch=ch, j=j: e.tensor_copy(out=qT[0:64, 0, ch, j * 512:(j + 1) * 512], in_=ps[1][0:64, :]),
                             reads=[PK(1), 'qTz'], writes=[('qT', j, ch, 0)])
                        S.op('act', lambda e, ch=ch, j=j: e.activation(out=qT[64:128, 1, ch, j * 512:(j + 1) * 512], in_=ps[1][64:128, :],
                                                                   func=AF.Copy),
                             reads=[PK(1), 'qTz'], writes=[('qT', j, ch, 1)])
                for ti in range(NT + 1):
                    for kc in range(KC):
                        S.op('pe', lambda e, kc=kc, ti=ti, wb=wb: e.matmul(ps[2][:, 0:64], lhsT=xnT[:, kc, ti * P:(ti + 1) * P],
                                                                        rhs=wb[:, kc, 384:448], start=(kc == 0), stop=(kc == KC - 1)),
                             reads=xk(ti) + [wk[3]], writes=[PK(2)])
                    S.op('act', lambda e, ti=ti: e.activation(out=vext[:, ti, 0:64], in_=ps[2][:, 0:64], func=AF.Copy),
                         reads=[PK(2)], writes=[('vext', ti)])
                    if ti >= 1:
                        for kc in range(KC):
                            S.op('pe', lambda e, kc=kc, ti=ti, wb=wb: e.matmul(ps[3][:, 0:256], lhsT=xnT[:, kc, ti * P:(ti + 1) * P],
                                                                            rhs=wb[:, kc, 448:704], start=(kc == 0), stop=(kc == KC - 1)),
                                 reads=xk(ti) + [wk[4]], writes=[PK(3)])
                        S.op('act', lambda e: e.activation(out=gtmp[:], in_=ps[3][:, 0:256], func=AF.Exp, scale=-1.0),
                             reads=[PK(3)], writes=['gtmp'])
                        S.op('act', lambda e: e.activation(out=gtmp[:], in_=gtmp[:], func=AF.Ln, bias=1.0), reads=['gtmp'], writes=['gtmp'])
                        S.op('act', lambda e: e.activation(out=gtmp[:], in_=gtmp[:], func=AF.Exp, scale=-1.0), reads=['gtmp'], writes=['gtmp'])
                        S.op('dve', lambda e, ti=ti: e.tensor_tensor(out=ga[:, ti - 1, :], in0=ps[3][:, 0:256], in1=gtmp[:], op=ALU.mult),
                             reads=[PK(3), 'gtmp'], writes=[('ga', ti - 1)])
                for ti in range(NT):
                    q_r = [('qT', ti // 4, c_, h_) for c_ in range(2) for h_ in range(2)]
                    for kb in range(2):
                        kcol = (ti + kb) * P
                        bank = 4 + kb
                        for hp in range(2):
                            o_ap = ps[bank][:, hp * 256:(hp + 1) * 256]
                            S.op('pe', lambda e, hp=hp, kcol=kcol, o_ap=o_ap, ti=ti: e.matmul(
                                o_ap, lhsT=kT[:, kcol:kcol + P],
                                rhs=qT[:, hp, :, ti * P:(ti + 1) * P], start=True, stop=True),
                                 reads=q_r + [('kT', kcol // 512)], writes=[PK(bank)])
                        S.op('act', lambda e, kb=kb, bank=bank: e.activation(out=Pm[kb][:], in_=ps[bank][:], func=AF.Exp, scale=0.125),
                             reads=[PK(bank)], writes=[('Pm', kb)])
                        msk = mC if kb == 1 else (mP0 if ti == 0 else mP)
                        S.op('dve', lambda e, kb=kb, msk=msk: e.tensor_tensor(out=Pm[kb][:], in0=Pm[kb][:], in1=msk[:], op=ALU.mult),
                             reads=[('Pm', kb), 'mC', 'mP', 'mP0'], writes=[('Pm', kb)])
                    pso = ps[6].rearrange("p (a t) -> p a t", a=4)
                    for a in range(4):
                        for kb in range(2):
                            S.op('pe', lambda e, a=a, kb=kb, ti=ti, pso=pso: e.matmul(pso[:, a, 0:65], lhsT=Pm[kb][:, ((a % 2) * 2 + a // 2) * P:((a % 2) * 2 + a // 2 + 1) * P],
                                                                                   rhs=vext[:, ti + kb, :], start=(kb == 0), stop=(kb == 1)),
                                 reads=[('Pm', kb), ('vext', ti + kb), ('vext1',)], writes=[PK(6)])
                    S.op('dve', lambda e, pso=pso, g=g: e.tensor_tensor(out=den[:], in0=pso[:, :, 64], in1=esink[:, 4 * g:4 * g + 4], op=ALU.add),
                         reads=[PK(6), 'esink'], writes=['den'])
                    S.op('dve', lambda e: e.reciprocal(out=den[:], in_=den[:]), reads=['den'], writes=['den'])
                    for a in range(4):
                        S.op('dve', lambda e, a=a, pso=pso, ti=ti: e.scalar_tensor_tensor(
                            out=uat[:, a * 64:(a + 1) * 64], in0=pso[:, a, 0:64], scalar=den[:, a:a + 1],
                            in1=ga[:, ti, a * 64:(a + 1) * 64], op0=ALU.mult, op1=ALU.mult),
                             reads=[PK(6), 'den', ('ga', ti)], writes=[('uat', a)])
                    ut_ps = ps[7].bitcast(BF16).rearrange("p (c t) -> p c t", c=8)
                    for cc in range(2):
                        S.op('pe', lambda e, cc=cc, ut_ps=ut_ps: e.transpose(out=ut_ps[:, cc, :], in_=uat[:, cc * P:(cc + 1) * P], identity=ident[:]),
                             reads=[('uat', a) for a in range(4)] + ['ident'], writes=[PK(7)])
                    S.op('act', lambda e, ut_ps=ut_ps, g=g, ti=ti: e.activation(out=uaT[:, 2 * g:2 * g + 2, ti * P:(ti + 1) * P],
                                                                           in_=ut_ps[:, 0:2, :], func=AF.Copy),
                         reads=[PK(7)], writes=[('uaT', g, ti)])
        S.barrier()
        hgrn_phase(full=True)

        with contextlib.ExitStack() as ph:
            stg = sb("stgP", [P, KC, P], F32, ph)
            Wc = [sb("Wc%d" % i, [P, KC, 512], BF16, ph) for i in range(2)]
            Wo = sb("Wo", [P, KC, D], BF16, ph)
            mT = sb("mT", [P, KC, T], BF16, ph)
            sgL = [[sb("sg%d_%d" % (q_, i), [P, 512], F32, ph) for i in range(2)] for q_ in range(2)]
            m1L = [sb("m1_%d" % q_, [P, 512], F32, ph) for q_ in range(2)]
            xt = [sb("xtP%d" % i, [P, D], F32, ph) for i in range(2)]
            junk = sb("junkP", [P, D], BF16, ph)
            ss2 = sb("ss2", [P, 1], F32, ph)
            rs2 = sb("rs2", [P, 1], F32, ph)
            fnw = sb("fnw", [P, D], F32, ph)
            S.dma(fnw[:], fnw_d, writes=['fnw'])
            w_bh_v = w_bh.rearrange("(kc p) n -> p kc n", p=P)
            w_ba_v = w_ba.rearrange("(kc p) n -> p kc n", p=P)
            w_out_v = w_out.rearrange("(kc p) n -> p kc n", p=P)

            def load_chunk(mc):
                wb = Wc[mc % 2]
                srcs = [w_bh_v[:, :, mc * P:(mc + 1) * P], w_ba_v[:, :, mc * P:(mc + 1) * P],
                        w_in_v[:, :, OFF['mh'] + mc * P:OFF['mh'] + (mc + 1) * P],
                        w_in_v[:, :, OFF['ma'] + mc * P:OFF['ma'] + (mc + 1) * P]]
                for pi, s_ap in enumerate(srcs):
                    load_w(stg[:], wb[:, :, pi * P:(pi + 1) * P], s_ap, ('Wc', mc % 2, pi), 'stgP')

            def post_item(mc, j, wb, wk):
                par = (mc * NB + j) % 2
                pb = 4 * par
                sg = sgL[par]
                m1 = m1L[par]
                tok0 = P + j * 512
                rk = xkb(1 + 4 * j, 5 + 4 * j)
                for kc in range(KC):
                    S.op('pe', lambda e, kc=kc, wb=wb, j=j: e.matmul(ps[pb][:], lhsT=wb[:, kc, 0:P],
                                                                  rhs=uhT[:, kc, j * 512:(j + 1) * 512], start=(kc == 0), stop=(kc == KC - 1)),
                         reads=[wk[0]], writes=[PK(pb)])
                for kc in range(KC):
                    S.op('pe', lambda e, kc=kc, wb=wb, j=j: e.matmul(ps[pb + 1][:], lhsT=wb[:, kc, P:2 * P],
                                                                  rhs=uaT[:, kc, j * 512:(j + 1) * 512], start=(kc == 0), stop=(kc == KC - 1)),
                         reads=[wk[1]], writes=[PK(pb + 1)])
                for w in range(2):
                    for kc in range(KC):
                        S.op('pe', lambda e, kc=kc, wb=wb, w=w, tok0=tok0: e.matmul(
                            ps[pb + 2 + w][:], lhsT=wb[:, kc, (2 + w) * P:(3 + w) * P],
                            rhs=xnT[:, kc, tok0:tok0 + 512], start=(kc == 0), stop=(kc == KC - 1)),
                             reads=rk + [wk[2 + w]], writes=[PK(pb + 2 + w)])
                    S.op('act', lambda e, w=w: e.activation(out=sg[w][:], in_=ps[pb + 2 + w][:], func=AF.Exp, scale=-1.0),
                         reads=[PK(pb + 2 + w)], writes=[('sg', par, w)])
                    S.op('act', lambda e, w=w: e.activation(out=sg[w][:], in_=sg[w][:], func=AF.Ln, bias=1.0),
                         reads=[('sg', par, w)], writes=[('sg', par, w)])
                    S.op('act', lambda e, w=w: e.activation(out=sg[w][:], in_=sg[w][:], func=AF.Exp, scale=-1.0),
                         reads=[('sg', par, w)], writes=[('sg', par, w)])
                S.op('dve', lambda e: e.tensor_tensor(out=m1[:], in0=ps[pb][:], in1=sg[0][:], op=ALU.mult),
                     reads=[PK(pb), ('sg', par, 0)], writes=[('m1', par)])
                S.op('dve', lambda e: e.tensor_tensor(out=sg[1][:], in0=ps[pb + 1][:], in1=sg[1][:], op=ALU.mult),
                     reads=[PK(pb + 1), ('sg', par, 1)], writes=[('sg', par, 1)])
                S.op('dve', lambda e, mc=mc, j=j: e.tensor_tensor(out=mT[:, mc, j * 512:(j + 1) * 512], in0=m1[:], in1=sg[1][:], op=ALU.add),
                     reads=[('m1', par), ('sg', par, 1)], writes=[('mT', mc, j)])

            load_chunk(0)
            for mc in range(KC):
                if mc + 1 < KC:
                    load_chunk(mc + 1)
                wb = Wc[mc % 2]
                wk = [('Wc', mc % 2, pi) for pi in range(4)]
                for j in range(NB):
                    post_item(mc, j, wb, wk)

            for q in range(KC):
                load_w(stg[:], Wo[:, :, q * P:(q + 1) * P], w_out_v[:, :, q * P:(q + 1) * P], ('Wo', q // 4), 'stgP')
            for ti in range(NT):
                j = ti // 4
                b = ti % 2
                S.dma(xt[b][:], x_own[ti * P:(ti + 1) * P, :], writes=[('xtP', b)])
                for nh in range(2):
                    bank = 4 + (2 * ti + nh) % 4
                    for mc in range(KC):
                        S.op('pe', lambda e, mc=mc, nh=nh, ti=ti, bank=bank: e.matmul(
                            ps[bank][:], lhsT=mT[:, mc, ti * P:(ti + 1) * P], rhs=Wo[:, mc, nh * 512:(nh + 1) * 512],
                            start=(mc == 0), stop=(mc == KC - 1)),
                             reads=[('mT', m, j) for m in range(KC)] + [('Wo', nh)], writes=[PK(bank)])
                    S.op('dve', lambda e, nh=nh, b=b, bank=bank: e.tensor_tensor(out=xt[b][:, nh * 512:(nh + 1) * 512], in0=ps[bank][:],
                                                                            in1=xt[b][:, nh * 512:(nh + 1) * 512], op=ALU.add),
                         reads=[PK(bank), ('xtP', b)], writes=[('xtP', b)])
                S.op('act', lambda e, b=b: e.activation(out=junk[:], in_=xt[b][:], func=AF.Square, accum_out=ss2[:]),
                     reads=[('xtP', b)], writes=['junkP', 'ss2'])
                S.op('act', lambda e: e.activation(out=rs2[:], in_=ss2[:], func=AF.Ln, scale=1.0 / D, bias=epsb[:]),
                     reads=['ss2', 'epsb'], writes=['rs2'])
                S.op('act', lambda e: e.activation(out=rs2[:], in_=rs2[:], func=AF.Exp, scale=-0.5), reads=['rs2'], writes=['rs2'])
                S.op('dve', lambda e, b=b: e.scalar_tensor_tensor(out=xt[b][:], in0=xt[b][:], scalar=rs2[:, 0:1], in1=fnw[:],
                                                              op0=ALU.mult, op1=ALU.mult),
                     reads=[('xtP', b), 'rs2', 'fnw'], writes=[('xtP', b)])
                S.dma(out_d[ti * P:(ti + 1) * P, :], xt[b][:], reads=[('xtP', b)], writes=[('outst', ti)])
        S.finish('sp')
        S.barrier()
    return nc


def _host_inputs(x, norm_w, w_in, hgrn_lower_bound, hgrn_norm_w, w_branch_hgrn, attn_sinks,
                 w_branch_attn, w_out, final_norm_w, NT):
    f32 = np.float32
    bf = ml_dtypes.bfloat16
    x = np.asarray(x, f32)
    B, SEQ, _ = x.shape
    T = NT * P
    nseg = SEQ // T
    w_in0 = np.ascontiguousarray(np.asarray(w_in, f32)[0])
    wbh = np.ascontiguousarray(np.asarray(w_branch_hgrn, f32)[0])
    wba = np.ascontiguousarray(np.asarray(w_branch_attn, f32)[0])
    wo = np.ascontiguousarray(np.asarray(w_out, f32)[0])
    nw = np.ascontiguousarray(np.asarray(norm_w, f32)[0].reshape(KC, P).T)
    lbp = np.asarray(hgrn_lower_bound, f32)
    lbp = np.ascontiguousarray(np.concatenate([lbp[0].reshape(8, P).T, lbp[1].reshape(8, P).T], axis=1))
    hnw = np.ascontiguousarray(np.asarray(hgrn_norm_w, f32)[0].reshape(8, P).T)
    sinks = np.ascontiguousarray(np.broadcast_to(np.asarray(attn_sinks, f32)[0][None, :], (P, 16)))
    fnw = np.ascontiguousarray(np.broadcast_to(np.asarray(final_norm_w, f32)[None, :], (P, D)))
    ident = np.eye(P, dtype=f32).astype(bf)
    si = np.arange(P)[:, None]
    tj = np.arange(P)[None, :]
    mC = np.tile((si <= tj).astype(f32), (1, 4)).astype(bf)
    mPm = np.tile((si > tj).astype(f32), (1, 4)).astype(bf)
    zeros_m = np.zeros((P, 512), f32).astype(bf)
    in_maps = []
    ncores = B * nseg
    for c in range(ncores):
        b, s = divmod(c, nseg)
        xo = np.ascontiguousarray(x[b, s * T:(s + 1) * T, :])
        xh = np.ascontiguousarray(x[b, s * T - P:s * T, :]) if s > 0 else np.zeros((P, D), f32)
        xp = np.zeros((NPREV * T, D), f32)
        for k in range(NPREV):
            sp = s - NPREV + k
            if sp >= 0:
                xp[k * T:(k + 1) * T] = x[b, sp * T:(sp + 1) * T, :]
        in_maps.append(dict(x_own=xo, x_halo=xh, w_in=w_in0, w_bh=wbh, w_ba=wba, w_out=wo, nw=nw, lbp=lbp, hnw=hnw,
                            sinks=sinks, fnw=fnw, x_prev=xp, ident=ident, maskC=mC, maskP=mPm,
                            maskP0=(zeros_m if s == 0 else mPm)))
    return in_maps, B, nseg, T


def kernel(x, norm_w, w_in, hgrn_lower_bound, hgrn_norm_w, w_branch_hgrn, attn_sinks,
           w_branch_attn, w_out, final_norm_w):
    NT = 16
    in_maps, B, nseg, T = _host_inputs(x, norm_w, w_in, hgrn_lower_bound, hgrn_norm_w, w_branch_hgrn, attn_sinks,
                                       w_branch_attn, w_out, final_norm_w, NT)
    nc = build_nc(NT)
    res = run_bass_kernel_spmd(nc, in_maps, core_ids=list(range(NCORES)))
    out = np.zeros((B, nseg * T, D), np.float32)
    for c in range(NCORES):
        b, s = divmod(c, nseg)
        out[b, s * T:(s + 1) * T, :] = np.asarray(res.results[c]["out"], np.float32)
    return out
```

```python
import contextlib
import numpy as np
import ml_dtypes
import concourse.bass as bass
import concourse.mybir as mybir
from concourse.bass_utils import run_bass_kernel_spmd

F32 = mybir.dt.float32
BF16 = mybir.dt.bfloat16
AF = mybir.ActivationFunctionType
ALU = mybir.AluOpType

P = 128
D = 1024
KC = 8
DIN = 8704
NCORES = 8
SEG_PER_BATCH = 4
OFF = dict(hq=0, hf=1024, hi=2048, hg=3072, aq=4096, ak=5120, av=5376, ag=5632, mh=6656, ma=7680)
EPS = 1e-6
NPREV = 3


class Sched:
    ENG = ['pe', 'act', 'dve', 'pool', 'sp']
    EPOCH = 1500

    def __init__(self, nc, es, n_dma_sems=12):
        self.nc = nc
        self.es = es
        self.streams = {e: [] for e in self.ENG}
        self.sem = {e: es.enter_context(nc.semaphore('c_' + e)) for e in self.ENG}
        self.cnt = {e: 0 for e in self.ENG}
        self.waited = {e: {} for e in self.ENG}
        self.res = {}
        self.dsem = [es.enter_context(nc.semaphore('d%d' % i)) for i in range(n_dma_sems)]
        self.dval = [0] * n_dma_sems
        self.dnext = 0
        self.semname = {}
        for e in self.ENG:
            self.semname[id(self.sem[e])] = self.sem[e]
        for s in self.dsem:
            self.semname[id(s)] = s
        self.all_events = {}
        self.skip_sids = set()
        self.nops = 0
        self.pending = []
        self.engobj = dict(pe=nc.tensor, act=nc.scalar, dve=nc.vector, pool=nc.gpsimd, sp=nc.sync)

    def _wait(self, eng, ev):
        sid, val = ev
        if self.waited[eng].get(sid, 0) >= val:
            return
        self.waited[eng][sid] = val
        sem = self.semname[sid]
        self.engobj[eng].wait_ge(sem, val)

    def _deps(self, eng, reads, writes):
        deps = []
        own = id(self.sem[eng])
        for r in reads:
            st = self.res.get(r)
            if st and st['w']:
                deps.append(st['w'])
        for w in writes:
            st = self.res.get(w)
            if st:
                if st['w']:
                    deps.append(st['w'])
                for sid, val in st['r'].items():
                    deps.append((sid, val))
        for ev in deps:
            if eng == 'pe' and ev[0] == own:
                continue
            self._wait(eng, ev)

    def _mark(self, ev, reads, writes):
        for r in reads:
            st = self.res.setdefault(r, {'w': None, 'r': {}})
            st['r'][ev[0]] = max(st['r'].get(ev[0], 0), ev[1])
        for w in writes:
            self.res[w] = {'w': ev, 'r': {}}
        self.all_events[ev[0]] = max(self.all_events.get(ev[0], 0), ev[1])

    COST = dict(pe=0.12, act=0.45, dve=0.40, pool=1.0, sp=1.5)
    LAT = 0.4
    SLACK = 0.15

    class _Probe:
        def __init__(self):
            self.out = None

        def __getattr__(self, name):
            def rec(*a, **k):
                o = k.get('out', a[0] if a else None)
                if o is None:
                    o = k.get('ap')
                self.out = o
                return self
            return rec

    def _est(self, eng, fn):
        try:
            pr = Sched._Probe()
            fn(pr)
            shp = tuple(pr.out.shape)
            n = 1
            for s_ in shp[1:]:
                n *= int(s_)
        except Exception:
            return self.COST[eng]
        if eng == 'pe':
            return 0.06 + n / 2400.0
        if eng == 'act':
            return 0.20 + n * 0.0006
        if eng == 'dve':
            return 0.20 + n * 0.0008
        if eng == 'pool':
            return 0.3 + n * 0.003
        return self.COST[eng]

    def op(self, eng, fn, reads=(), writes=(), cost=None):
        self.pending.append(('op', eng, fn, list(reads), list(writes), self._est(eng, fn)))

    def dma(self, out, in_, reads=(), writes=(), eng='sp'):
        try:
            nel = 1
            for s_ in tuple(out.shape):
                nel *= int(s_)
            cost = 2.0 + nel * 4 / 250e3
        except Exception:
            cost = self.COST['sp']
        self.pending.append(('dma', eng, (out, in_), list(reads), list(writes), cost))

    def flush(self):
        ops = self.pending
        self.pending = []
        n = len(ops)
        if n == 0:
            return
        lastw = {}
        readers = {}
        preds = [set() for _ in range(n)]
        for i, (kind, eng, fn, reads, writes, cost) in enumerate(ops):
            pk = [r for r in reads if isinstance(r, tuple) and r and r[0] == 'ps']
            rd = [r for r in reads if r not in pk]
            wr = list(writes) + [r for r in pk if r not in writes]
            for r in rd:
                if r in lastw:
                    preds[i].add(lastw[r])
            for w in wr:
                if w in lastw:
                    preds[i].add(lastw[w])
                for j in readers.get(w, ()):
                    preds[i].add(j)
            for r in rd:
                readers.setdefault(r, []).append(i)
            for w in wr:
                lastw[w] = i
                readers[w] = []
            preds[i].discard(i)
        succs = [[] for _ in range(n)]
        npred = [0] * n
        for i in range(n):
            npred[i] = len(preds[i])
            for j in preds[i]:
                succs[j].append(i)
        fin = [0.0] * n
        ready_t = [0.0] * n
        avail = {e: 0.0 for e in self.ENG}
        ready = [i for i in range(n) if npred[i] == 0]
        order = []
        last_eng_idx = {e: -1 for e in self.ENG}
        rank = [0.0] * n
        for i in range(n - 1, -1, -1):
            m_ = 0.0
            for s in succs[i]:
                if rank[s] > m_:
                    m_ = rank[s]
            rank[i] = ops[i][5] + self.LAT + m_
        while ready:
            sts = {}
            mn = None
            for i in ready:
                st = max(ready_t[i], avail[ops[i][1]])
                sts[i] = st
                if mn is None or st < mn:
                    mn = st
            best = None
            for i in ready:
                if sts[i] <= mn + self.SLACK:
                    key = (-rank[i], i)
                    if best is None or key < best[0]:
                        best = (key, i)
            i = best[1]
            st = sts[i]
            ready.remove(i)
            eng = ops[i][1]
            fin[i] = st + ops[i][5]
            avail[eng] = fin[i]
            order.append(i)
            for s in succs[i]:
                lat = 0.0 if (ops[s][1] == eng == 'pe') else self.LAT
                ready_t[s] = max(ready_t[s], fin[i] + lat)
                npred[s] -= 1
                if npred[s] == 0:
                    ready.append(s)
        assert len(order) == n
        for i in order:
            kind, eng, fn, reads, writes, cost = ops[i]
            if kind == 'op':
                self._op_now(eng, fn, reads, writes)
            else:
                self._dma_now(fn[0], fn[1], reads, writes, eng)

    def _op_now(self, eng, fn, reads=(), writes=()):
        self.nops += 1
        pk = [r for r in reads if isinstance(r, tuple) and r and r[0] == 'ps']
        if pk:
            reads = [r for r in reads if r not in pk]
            writes = list(writes) + [r for r in pk if r not in writes]
        self._deps(eng, reads, writes)
        if self.cnt[eng] >= self.EPOCH:
            ns = self.es.enter_context(self.nc.semaphore('c_%s_%d' % (eng, len(self.semname))))
            self.semname[id(ns)] = ns
            self.sem[eng] = ns
            self.cnt[eng] = 0
        self.cnt[eng] += 1
        sem = self.sem[eng]
        ev = (id(sem), self.cnt[eng])
        fn(self.engobj[eng]).then_inc(sem, 1)
        self._mark(ev, reads, writes)
        return ev

    def _dma_now(self, out, in_, reads=(), writes=(), eng='sp'):
        k = self.dnext
        self.dnext = (self.dnext + 1) % len(self.dsem)
        sem = self.dsem[k]
        if self.dval[k] > 0:
            self._wait(eng, (id(sem), self.dval[k]))
        self._deps(eng, reads, writes)
        self.dval[k] += 16
        ev = (id(sem), self.dval[k])
        self.engobj[eng].dma_start(out=out, in_=in_).then_inc(sem, 16)
        self._mark(ev, reads, writes)
        return ev

    def custom(self, eng, fn, sem, val, reads=(), writes=()):
        self._deps(eng, reads, writes)
        self.semname[id(sem)] = sem
        self.skip_sids.add(id(sem))
        ev = (id(sem), val)
        fn(self.engobj[eng])
        self._mark(ev, reads, writes)
        return ev

    def barrier(self):
        self.flush()
        for e in self.ENG:
            for sid, val in list(self.all_events.items()):
                if sid in self.skip_sids:
                    continue
                self._wait(e, (sid, val))
        keep = {k: {'w': v['w'], 'r': {}} for k, v in self.res.items() if v['w'] and v['w'][0] in self.skip_sids}
        self.res = keep

    def finish(self, eng='sp'):
        self.flush()
        for sid, val in list(self.all_events.items()):
            self._wait(eng, (sid, val))

    def emit(self):
        nc = self.nc
        with nc.Block() as block:
            @block.tensor
            def _(e):
                for f in self.streams['pe']:
                    f(e)

            @block.scalar
            def _(e):
                for f in self.streams['act']:
                    f(e)

            @block.vector
            def _(e):
                for f in self.streams['dve']:
                    f(e)

            @block.gpsimd
            def _(e):
                for f in self.streams['pool']:
                    f(e)

            @block.sync
            def _(e):
                for f in self.streams['sp']:
                    f(e)


def build_nc(NT, stop=None, use_cc=True, limit=None, cc_first=False, pad_kb=0):
    T = NT * P
    NB = NT // 4
    TA = (NT + 1) * P
    nc = bass.Bass("TRN2", target_bir_lowering=False)

    def din(name, shape, dt=F32):
        return nc.dram_tensor(name, list(shape), dt, kind="ExternalInput").ap()

    x_own = din("x_own", [T, D])
    x_halo = din("x_halo", [P, D])
    w_in = din("w_in", [D, DIN])
    w_bh = din("w_bh", [D, D])
    w_ba = din("w_ba", [D, D])
    w_out = din("w_out", [D, D])
    nw_d = din("nw", [P, KC])
    lbp_d = din("lbp", [P, 16])
    hnw_d = din("hnw", [P, 8])
    sink_d = din("sinks", [P, 16])
    fnw_d = din("fnw", [P, D])
    ident_d = din("ident", [P, P], BF16)
    mC_d = din("maskC", [P, 512], BF16)
    mP_d = din("maskP", [P, 512], BF16)
    mP0_d = din("maskP0", [P, 512], BF16)
    out_d = nc.dram_tensor("out", [T, D], F32, kind="ExternalOutput").ap()
    x_prev = din("x_prev", [NPREV * T, D])

    w_in_v = w_in.rearrange("(kc p) n -> p kc n", p=P)

    with contextlib.ExitStack() as es:
        S = Sched(nc, es)
        build_nc.last_sched = S

        uniq = [0]

        def sb(name, shape, dt=F32, stack=None):
            uniq[0] += 1
            return (stack or es).enter_context(nc.sbuf_tensor("sb%d_%s" % (uniq[0], name), list(shape), dt))

        if pad_kb:
            sb("padding", [P, pad_kb * 256], F32)
        ps = [es.enter_context(nc.psum_tensor("ps%d" % i, [P, 512], F32)) for i in range(8)]

        def PK(i):
            return ('ps', i)

        xnT = sb("xnT", [P, KC, TA], BF16)
        ident = sb("ident", [P, P], BF16)
        mC = sb("mC", [P, 512], BF16)
        mP = sb("mP", [P, 512], BF16)
        mP0 = sb("mP0", [P, 512], BF16)
        nw = sb("nw", [P, KC])
        lbp = sb("lbp", [P, 16])
        hnw = sb("hnw", [P, 8])
        esink = sb("esink", [P, 16])
        lb = sb("lb", [P, 8])
        ln1mlb = sb("ln1mlb", [P, 8])
        tmp8 = sb("tmp8", [P, 8])
        ones = sb("ones", [P, P])
        epsb = sb("epsb", [P, 1])
        S_in = sb("S_in", [P, 8, P])

        def xk(ti):
            return [('xnT', ti, c) for c in range(KC)]

        def xkb(t0, t1):
            r = []
            for ti in range(t0, t1):
                r += xk(ti)
            return r

        for (dst, src, key) in [(ident, ident_d, 'ident'), (mC, mC_d, 'mC'), (mP, mP_d, 'mP'), (mP0, mP0_d, 'mP0'),
                                (nw, nw_d, 'nw'), (lbp, lbp_d, 'lbp'), (hnw, hnw_d, 'hnw'), (esink, sink_d, 'esink')]:
            S.dma(dst[:], src, writes=[key])
        S.op('dve', lambda e: e.memset(ones[:], 1.0), writes=['ones'])
        S.op('dve', lambda e: e.memset(epsb[:], EPS), writes=['epsb'])
        S.op('dve', lambda e: e.memset(S_in[:], 0.0), writes=[('S_in', h_) for h_ in range(8)])
        S.op('dve', lambda e: e.tensor_tensor(out=tmp8[:], in0=lbp[:, 8:16], in1=lbp[:, 0:8], op=ALU.subtract),
             reads=['lbp'], writes=['tmp8'])
        S.op('act', lambda e: e.activation(out=tmp8[:], in_=tmp8[:], func=AF.Exp), reads=['tmp8'], writes=['tmp8'])
        S.op('act', lambda e: e.activation(out=tmp8[:], in_=tmp8[:], func=AF.Ln, bias=1.0), reads=['tmp8'], writes=['tmp8'])
        S.op('act', lambda e: e.activation(out=lb[:], in_=tmp8[:], func=AF.Exp, scale=-1.0), reads=['tmp8'], writes=['lb'])
        S.op('act', lambda e: e.activation(out=ln1mlb[:], in_=lb[:], func=AF.Ln, scale=-1.0, bias=1.0),
             reads=['lb'], writes=['ln1mlb'])
        S.op('act', lambda e: e.activation(out=esink[:], in_=esink[:], func=AF.Exp), reads=['esink'], writes=['esink'])

        lw_cnt = [0]

        def load_w(stg, dst_ap, src_ap, key, skey):
            S.dma(stg, src_ap, writes=[skey])
            lw_cnt[0] += 1
            if lw_cnt[0] % 3 == 0:
                S.op('act', lambda e: e.activation(out=dst_ap, in_=stg, func=AF.Copy), reads=[skey], writes=[key], cost=1.5)
            else:
                S.op('pool', lambda e: e.tensor_copy(out=dst_ap, in_=stg), reads=[skey], writes=[key], cost=2.5)

        def stage_a_alloc(ph):
            return dict(xt=[sb("xt%d" % i, [P, D], F32, ph) for i in range(3)],
                        xs=[sb("xs%d" % i, [P, D], BF16, ph) for i in range(2)],
                        junk=sb("junkA", [P, D], BF16, ph),
                        ssq=sb("ssq", [P, NT + 1], F32, ph),
                        lnms=sb("lnms", [P, NT + 1], F32, ph),
                        rstd=sb("rstd", [P, NT + 1], F32, ph))

        def stage_a(tiles, ctx=None):
            with contextlib.ExitStack() as ph:
                A = ctx if ctx is not None else stage_a_alloc(ph)
                xt, xs, junk, ssq, lnms, rstd = A['xt'], A['xs'], A['junk'], A['ssq'], A['lnms'], A['rstd']
                S.op('dve', lambda e: e.memset(ssq[:], 0.0), writes=[('ssq', i) for i in range(NT + 1)])
                for n_, (ti, src_ap) in enumerate(tiles):
                    b = n_ % 2
                    b3 = n_ % 3
                    S.dma(xt[b3][:], src_ap, writes=[('xt', b3)])
                    S.op('act', lambda e, b3=b3, ti=ti: e.activation(out=junk[:], in_=xt[b3][:], func=AF.Square,
                                                                accum_out=ssq[:, ti:ti + 1]),
                         reads=[('xt', b3), ('ssq', ti)], writes=['junkA', ('ssq', ti)], cost=1.1)
                    S.op('act', lambda e, ti=ti: e.activation(out=lnms[:, ti:ti + 1], in_=ssq[:, ti:ti + 1], func=AF.Ln,
                                                          scale=1.0 / D, bias=epsb[:]),
                         reads=[('ssq', ti), 'epsb'], writes=[('lnms', ti)])
                    S.op('act', lambda e, ti=ti: e.activation(out=rstd[:, ti:ti + 1], in_=lnms[:, ti:ti + 1], func=AF.Exp,
                                                          scale=-0.5),
                         reads=[('lnms', ti)], writes=[('rstd', ti)])
                    S.op('dve', lambda e, b=b, b3=b3, ti=ti: e.tensor_scalar(out=xs[b][:], in0=xt[b3][:], scalar1=rstd[:, ti:ti + 1],
                                                                  scalar2=None, op0=ALU.mult),
                         reads=[('xt', b3), ('rstd', ti)], writes=[('xs', b)], cost=0.8)
                    pst = ps[b].bitcast(BF16).rearrange("p (c t) -> p c t", c=KC)
                    for c in range(KC):
                        S.op('pe', lambda e, b=b, c=c, pst=pst: e.transpose(out=pst[:, c, :], in_=xs[b][:, c * P:(c + 1) * P],
                                                                        identity=ident[:]),
                             reads=[('xs', b), 'ident'], writes=[PK(b)])
                    for c in range(KC):
                        dst = xnT[:, c, ti * P:(ti + 1) * P]
                        if c % 2 == 0:
                            S.op('dve', lambda e, c=c, pst=pst, dst=dst: e.tensor_scalar(out=dst, in0=pst[:, c, :],
                                                                                    scalar1=nw[:, c:c + 1], scalar2=None,
                                                                                    op0=ALU.mult),
                                 reads=[PK(b), 'nw'], writes=[('xnT', ti, c)])
                        else:
                            S.op('act', lambda e, c=c, pst=pst, dst=dst: e.activation(out=dst, in_=pst[:, c, :], func=AF.Copy,
                                                                                 scale=nw[:, c:c + 1]),
                                 reads=[PK(b), 'nw'], writes=[('xnT', ti, c)])
            if ctx is None:
                S.barrier()

        def hgrn_phase(full):
            with contextlib.ExitStack() as ph:
                tag = 'F' if full else 'S'
                stg = sb("stg" + tag, [P, KC, 512], F32, ph)
                Wh = [sb("Wh%s%d" % (tag, i), [P, KC, 512], BF16, ph) for i in range(4)]
                e_tL = [sb("e_tF%d" % i_, [P, 512], F32, ph) for i_ in range(2)]
                L1L = [sb("L1F%d" % i_, [P, 512], F32, ph) for i_ in range(2)]
                g_tL = [sb("g_tF%d" % i_, [P, 512], F32, ph) for i_ in range(2)]
                b_tL = [sb("b_tF%d" % i_, [P, 512], F32, ph) for i_ in range(2)]
                t_tL = [sb("t_tF%d" % i_, [P, 512], F32, ph) for i_ in range(2)]
                ktTL = [sb("ktTF%d" % i_, [P, 512], BF16, ph) for i_ in range(2)]
                vbfL = [sb("vbfF%d" % i_, [P, 4, P], BF16, ph) for i_ in range(2)]
                ktokL = [sb("ktokH%d" % i_, [P, P], BF16, ph) for i_ in range(2)]
                S32L = [sb("S32H%d" % i_, [P, P], F32, ph) for i_ in range(2)]
                StmpL = [sb("StmpH%d" % i_, [P, P], F32, ph) for i_ in range(2)]
                decL = [sb("decH%d" % i_, [P, 1], F32, ph) for i_ in range(2)]
                sumb = sb("sumb" + tag, [P, 8], F32, ph)
                if full:
                    eqL = [sb("eqF%d" % i_, [P, 512], F32, ph) for i_ in range(2)]
                    qtTL = [sb("qtTF%d" % i_, [P, 512], BF16, ph) for i_ in range(2)]
                    gateL = [sb("gateF%d" % i_, [P, 4, P], F32, ph) for i_ in range(2)]
                    SbfL = [sb("SbfH%d" % i_, [P, P], BF16, ph) for i_ in range(2)]
                    ATmL = [sb("ATmH%d" % i_, [P, P], BF16, ph) for i_ in range(2)]
                    junkL = [sb("junkHH%d" % i_, [P, P], BF16, ph) for i_ in range(2)]
                    ssoL = [sb("ssoH%d" % i_, [P, 1], F32, ph) for i_ in range(2)]
                    rsoL = [sb("rsoH%d" % i_, [P, 1], F32, ph) for i_ in range(2)]
                    utokL = [sb("utokH%d" % i_, [P, P], BF16, ph) for i_ in range(2)]

                def load_head(h):
                    wb = Wh[h % 4]
                    for gi, nm in enumerate(['hq', 'hf', 'hi', 'hg']):
                        if not full and nm in ('hq', 'hg'):
                            continue
                        c0 = OFF[nm] + h * P
                        load_w(stg[:, :, gi * P:(gi + 1) * P], wb[:, :, gi * P:(gi + 1) * P], w_in_v[:, :, c0:c0 + P],
                               ('Wh', h % 4, gi), ('stg', gi))

                def do_block(h, j, fb, wb, wk):
                    hp = h % 2
                    ktok = ktokL[hp]
                    S32 = S32L[hp]
                    Stmp = StmpL[hp]
                    dec = decL[hp]
                    Sbf = SbfL[hp]
                    ATm = ATmL[hp]
                    junk = junkL[hp]
                    sso = ssoL[hp]
                    rso = rsoL[hp]
                    utok = utokL[hp]
                    e_t = e_tL[fb]
                    L1 = L1L[fb]
                    g_t = g_tL[fb]
                    b_t = b_tL[fb]
                    t_t = t_tL[fb]
                    ktT = ktTL[fb]
                    vbf = vbfL[fb]
                    eq = eqL[fb]
                    qtT = qtTL[fb]
                    gate = gateL[fb]
                    tok0 = P + j * 512
                    rk = xkb(1 + 4 * j, 5 + 4 * j)
                    for kc in range(KC):
                        S.op('pe', lambda e, kc=kc, wb=wb, tok0=tok0: e.matmul(ps[0][:], lhsT=wb[:, kc, P:2 * P],
                                                                            rhs=xnT[:, kc, tok0:tok0 + 512],
                                                                            start=(kc == 0), stop=(kc == KC - 1)),
                             reads=rk + [wk[1]], writes=[PK(0)])
                    if full:
                        for kc in range(KC):
                            S.op('pe', lambda e, kc=kc, wb=wb, tok0=tok0: e.matmul(ps[1][:], lhsT=wb[:, kc, 0:P],
                                                                                rhs=xnT[:, kc, tok0:tok0 + 512],
                                                                                start=(kc == 0), stop=(kc == KC - 1)),
                                 reads=rk + [wk[0]], writes=[PK(1)])
                    for tl in range(4):
                        c0 = tok0 + tl * P
                        bank = 2 + tl // 2
                        half = tl % 2
                        if full:
                            o_ap = ps[bank][:, half * 256:(half + 1) * 256]
                            r_sl = slice(2 * P, 4 * P)
                            rkk = [wk[2], wk[3]]
                        else:
                            o_ap = ps[bank][:, half * 256:half * 256 + P]
                            r_sl = slice(2 * P, 3 * P)
                            rkk = [wk[2]]
                        for kc in range(KC):
                            S.op('pe', lambda e, kc=kc, wb=wb, c0=c0, o_ap=o_ap, r_sl=r_sl: e.matmul(
                                o_ap, lhsT=xnT[:, kc, c0:c0 + P], rhs=wb[:, kc, r_sl],
                                start=(kc == 0), stop=(kc == KC - 1)),
                                 reads=xk(1 + 4 * j + tl) + rkk, writes=[PK(bank)])
                    S.op('act', lambda e: e.activation(out=e_t[:], in_=ps[0][:], func=AF.Exp, scale=-1.0),
                         reads=[PK(0)], writes=[('e_t', fb)])
                    S.op('act', lambda e: e.activation(out=L1[:], in_=e_t[:], func=AF.Ln, bias=1.0),
                         reads=[('e_t', fb)], writes=[('L1', fb)])
                    S.op('act', lambda e, h=h: e.activation(out=g_t[:], in_=e_t[:], func=AF.Ln, scale=lb[:, h:h + 1],
                                                        bias=1.0),
                         reads=[('e_t', fb), 'lb'], writes=[('g_t', fb)])
                    S.op('dve', lambda e: e.tensor_tensor(out=g_t[:], in0=g_t[:], in1=L1[:], op=ALU.subtract),
                         reads=[('g_t', fb), ('L1', fb)], writes=[('g_t', fb)])
                    for tl in range(4):
                        S.op('dve', lambda e, tl=tl: e.tensor_tensor_scan(out=b_t[:, tl * P:(tl + 1) * P], data0=ones[:],
                                                                     data1=g_t[:, tl * P:(tl + 1) * P], initial=0.0,
                                                                     op0=ALU.mult, op1=ALU.add),
                             reads=[('g_t', fb), 'ones'], writes=[('b_t', fb, tl)])
                    bk = [('b_t', fb, tl) for tl in range(4)]
                    S.op('dve', lambda e: e.scalar_tensor_tensor(out=t_t[:], in0=ps[0][:], scalar=-1.0, in1=L1[:],
                                                              op0=ALU.mult, op1=ALU.subtract),
                         reads=[PK(0), ('L1', fb)], writes=[('t_t', fb)])
                    S.op('dve', lambda e: e.tensor_tensor(out=t_t[:], in0=t_t[:], in1=b_t[:], op=ALU.subtract),
                         reads=[('t_t', fb)] + bk, writes=[('t_t', fb)])
                    S.op('act', lambda e, h=h: e.activation(out=ktT[:], in_=t_t[:], func=AF.Exp,
                                                        bias=ln1mlb[:, h:h + 1]),
                         reads=[('t_t', fb), 'ln1mlb'], writes=[('ktT', fb)])
                    if full:
                        S.op('act', lambda e: e.activation(out=eq[:], in_=ps[1][:], func=AF.Exp, scale=-1.0),
                             reads=[PK(1)], writes=[('eq', fb)])
                        S.op('act', lambda e: e.activation(out=eq[:], in_=eq[:], func=AF.Ln, bias=1.0),
                             reads=[('eq', fb)], writes=[('eq', fb)])
                        S.op('dve', lambda e: e.tensor_tensor(out=eq[:], in0=b_t[:], in1=eq[:], op=ALU.subtract),
                             reads=[('eq', fb)] + bk, writes=[('eq', fb)])
                        S.op('act', lambda e: e.activation(out=eq[:], in_=eq[:], func=AF.Exp), reads=[('eq', fb)], writes=[('eq', fb)])
                        S.op('dve', lambda e: e.tensor_tensor(out=qtT[:], in0=ps[1][:], in1=eq[:], op=ALU.mult),
                             reads=[PK(1), ('eq', fb)], writes=[('qtT', fb)])
                        for bank in (2, 3):
                            gsrc = ps[bank].rearrange("p (t c) -> p t c", t=2)[:, :, P:2 * P]
                            gdst = gate[:, (bank - 2) * 2:(bank - 2) * 2 + 2, :]
                            S.op('act', lambda e, gsrc=gsrc, gdst=gdst: e.activation(out=gdst, in_=gsrc, func=AF.Exp, scale=-1.0),
                                 reads=[PK(bank)], writes=[('gate', fb, bank)])
                            S.op('act', lambda e, gdst=gdst: e.activation(out=gdst, in_=gdst, func=AF.Ln, bias=1.0),
                                 reads=[('gate', fb, bank)], writes=[('gate', fb, bank)])
                            S.op('act', lambda e, gdst=gdst: e.activation(out=gdst, in_=gdst, func=AF.Exp, scale=-1.0),
                                 reads=[('gate', fb, bank)], writes=[('gate', fb, bank)])
                            S.op('dve', lambda e, gsrc=gsrc, gdst=gdst: e.tensor_tensor(out=gdst, in0=gsrc, in1=gdst, op=ALU.mult),
                                 reads=[PK(bank), ('gate', fb, bank)], writes=[('gate', fb, bank)])
                    for bank in (2, 3):
                        vsrc = ps[bank].rearrange("p (t c) -> p t c", t=2)[:, :, 0:P]
                        vdst = vbf[:, (bank - 2) * 2:(bank - 2) * 2 + 2, :]
                        S.op('act', lambda e, vsrc=vsrc, vdst=vdst: e.activation(out=vdst, in_=vsrc, func=AF.Copy),
                             reads=[PK(bank)], writes=[('vbf', fb, bank)])
                    for tl in range(4):
                        ti = 4 * j + tl
                        sl = slice(tl * P, (tl + 1) * P)
                        vk = ('vbf', fb, 2 + tl // 2)
                        v_ap = vbf[:, tl, :]
                        S.op('act', lambda e, tl=tl: e.activation(out=dec[:], in_=b_t[:, tl * P + P - 1:tl * P + P], func=AF.Exp),
                             reads=[('b_t', fb, tl)], writes=[('dec', hp)])
                        kt_ps = ps[6].bitcast(BF16)[:, 0:P]
                        S.op('pe', lambda e, sl=sl, kt_ps=kt_ps: e.transpose(out=kt_ps, in_=ktT[:, sl], identity=ident[:]),
                             reads=[('ktT', fb), 'ident'], writes=[PK(6)])
                        S.op('dve', lambda e, kt_ps=kt_ps: e.tensor_copy(out=ktok[:], in_=kt_ps), reads=[PK(6)], writes=[('ktok', hp)])
                        if full:
                            S.op('pe', lambda e, sl=sl: e.matmul(ps[4][:, 0:P], lhsT=qtT[:, sl], rhs=Sbf[:], start=True, stop=False),
                                 reads=[('qtT', fb), ('Sbf', hp)], writes=[PK(4)])
                            S.op('pe', lambda e, sl=sl: e.matmul(ps[5][:, 0:P], lhsT=ktT[:, sl], rhs=qtT[:, sl], start=True, stop=True),
                                 reads=[('qtT', fb), ('ktT', fb)], writes=[PK(5)])
                            S.op('dve', lambda e: e.tensor_tensor(out=ATm[:], in0=ps[5][:, 0:P], in1=mC[:, 0:P], op=ALU.mult),
                                 reads=[PK(5), 'mC'], writes=[('ATm', hp)])
                            S.op('pe', lambda e, v_ap=v_ap: e.matmul(ps[4][:, 0:P], lhsT=ATm[:], rhs=v_ap, start=False, stop=True),
                                 reads=[('ATm', hp), vk], writes=[PK(4)])
                        S.op('pe', lambda e, v_ap=v_ap: e.matmul(ps[7][:, 0:P], lhsT=ktok[:], rhs=v_ap, start=True, stop=True),
                             reads=[('ktok', hp), vk], writes=[PK(7)])
                        S.op('dve', lambda e: e.tensor_scalar(out=Stmp[:], in0=S32[:], scalar1=dec[:, 0:1], scalar2=None, op0=ALU.mult),
                             reads=[('S32', hp), ('dec', hp)], writes=[('Stmp', hp)])
                        S.op('dve', lambda e: e.scalar_tensor_tensor(out=S32[:], in0=ps[7][:, 0:P], scalar=dec[:, 0:1], in1=Stmp[:],
                                                                  op0=ALU.mult, op1=ALU.add),
                             reads=[PK(7), ('dec', hp), ('Stmp', hp)], writes=[('S32', hp)])
                        if full:
                            S.op('act', lambda e: e.activation(out=junk[:], in_=ps[4][:, 0:P], func=AF.Square, accum_out=sso[:]),
                                 reads=[PK(4)], writes=[('junkH', hp), ('sso', hp)])
                            S.op('act', lambda e: e.activation(out=rso[:], in_=sso[:], func=AF.Ln, scale=1.0 / P, bias=epsb[:]),
                                 reads=[('sso', hp), 'epsb'], writes=[('rso', hp)])
                            S.op('act', lambda e: e.activation(out=rso[:], in_=rso[:], func=AF.Exp, scale=-0.5),
                                 reads=[('rso', hp)], writes=[('rso', hp)])
                            S.op('dve', lambda e, tl=tl: e.scalar_tensor_tensor(out=utok[:], in0=ps[4][:, 0:P], scalar=rso[:, 0:1],
                                                                            in1=gate[:, tl, :], op0=ALU.mult, op1=ALU.mult),
                                 reads=[PK(4), ('rso', hp), ('gate', fb, 2 + tl // 2)], writes=[('utok', hp)])
                            S.op('dve', lambda e: e.tensor_copy(out=Sbf[:], in_=S32[:]), reads=[('S32', hp)], writes=[('Sbf', hp)])
                            ut_ps = ps[6].bitcast(BF16)[:, 0:P]
                            S.op('pe', lambda e, ut_ps=ut_ps: e.transpose(out=ut_ps, in_=utok[:], identity=ident[:]),
                                 reads=[('utok', hp), 'ident'], writes=[PK(6)])
                            S.op('act', lambda e, ut_ps=ut_ps, h=h, ti=ti: e.activation(out=uhT[:, h, ti * P:(ti + 1) * P], in_=ut_ps,
                                                                                   func=AF.Copy, scale=hnw[:, h:h + 1]),
                                 reads=[PK(6), 'hnw'], writes=[('uhT', h, ti)])

                load_head(0)
                load_head(1)
                cnt_ = [0]
                for pr in range(4):
                    for hh_ in (2 * pr + 2, 2 * pr + 3):
                        if hh_ < 8:
                            load_head(hh_)
                    for h in (2 * pr, 2 * pr + 1):
                        hp = h % 2
                        S.op('dve', lambda e, h=h, hp=hp: e.tensor_copy(out=S32L[hp][:], in_=S_in[:, h, :]), reads=[('S_in', h)], writes=[('S32', hp)])
                        S.op('dve', lambda e, h=h, hp=hp: e.tensor_copy(out=SbfL[hp][:], in_=S_in[:, h, :]), reads=[('S_in', h)], writes=[('Sbf', hp)])
                    for j in range(NB):
                        for h in (2 * pr, 2 * pr + 1):
                            do_block(h, j, cnt_[0] % 2, Wh[h % 4], [('Wh', h % 4, gi) for gi in range(4)])
                            cnt_[0] += 1
            S.barrier()

        def summary_alloc(ph):
            return dict(
                e_t=[sb("pe_t%d" % i, [P, 512], F32, ph) for i in range(4)],
                L1=[sb("pL1%d" % i, [P, 512], F32, ph) for i in range(4)],
                g_t=[sb("pg_t%d" % i, [P, 512], F32, ph) for i in range(4)],
                b_t=[sb("pb_t%d" % i, [P, 512], F32, ph) for i in range(4)],
                t_t=[sb("pt_t%d" % i, [P, 512], F32, ph) for i in range(4)],
                ktT=[sb("pktT%d" % i, [P, 512], BF16, ph) for i in range(6)],
                decs=[sb("pdecs%d" % i, [P, 4], F32, ph) for i in range(6)],
                vall=[sb("pvall%d" % i, [P, 4, D], BF16, ph) for i in range(2)],
                ktok=[sb("pktok%d" % i, [P, P], BF16, ph) for i in range(2)],
                tmpS=[sb("ptmpS%d" % i, [P, P], F32, ph) for i in range(2)])

        def summary_pass(Wf, Wi, ctx=None):
            with contextlib.ExitStack() as ph:
                C = ctx if ctx is not None else summary_alloc(ph)
                e_t, L1, g_t, b_t, t_t, ktT, decs, vall, ktok, tmpS = (C[k_] for k_ in
                    ('e_t', 'L1', 'g_t', 'b_t', 't_t', 'ktT', 'decs', 'vall', 'ktok', 'tmpS'))

                def vproj(j):
                    jb = j % 2
                    tok0 = P + j * 512
                    for tl in range(4):
                        c0 = tok0 + tl * P
                        for half in range(2):
                            bank = 2 + (2 * tl + half) % 2
                            for kc in range(KC):
                                S.op('pe', lambda e, kc=kc, c0=c0, half=half, bank=bank: e.matmul(
                                    ps[bank][:], lhsT=xnT[:, kc, c0:c0 + P], rhs=Wi[:, kc, half * 512:(half + 1) * 512],
                                    start=(kc == 0), stop=(kc == KC - 1)),
                                     reads=xk(1 + 4 * j + tl) + [('Wi', half)], writes=[PK(bank)], cost=0.22)
                            S.op('act', lambda e, jb=jb, tl=tl, half=half, bank=bank: e.activation(
                                out=vall[jb][:, tl, half * 512:(half + 1) * 512], in_=ps[bank][:], func=AF.Copy),
                                 reads=[PK(bank)], writes=[('vall', jb, tl, half)])

                def front(idx, j, h):
                    pb = idx % 2
                    bf = idx % 4
                    hb = idx % 6
                    tok0 = P + j * 512
                    rk = xkb(1 + 4 * j, 5 + 4 * j)
                    for kc in range(KC):
                        S.op('pe', lambda e, kc=kc, tok0=tok0, h=h, pb=pb: e.matmul(
                            ps[pb][:], lhsT=Wf[:, kc, h * P:(h + 1) * P], rhs=xnT[:, kc, tok0:tok0 + 512],
                            start=(kc == 0), stop=(kc == KC - 1)),
                             reads=rk + [('Wf', h // 4)], writes=[PK(pb)], cost=0.22)
                    S.op('act', lambda e, bf=bf, pb=pb: e.activation(out=e_t[bf][:], in_=ps[pb][:], func=AF.Exp, scale=-1.0),
                         reads=[PK(pb)], writes=[('e_t', bf)])
                    S.op('dve', lambda e, bf=bf, pb=pb: e.tensor_scalar(out=t_t[bf][:], in0=ps[pb][:], scalar1=-1.0, scalar2=None, op0=ALU.mult),
                         reads=[PK(pb)], writes=[('t_t', bf)])
                    S.op('act', lambda e, bf=bf: e.activation(out=L1[bf][:], in_=e_t[bf][:], func=AF.Ln, bias=1.0),
                         reads=[('e_t', bf)], writes=[('L1', bf)])
                    S.op('act', lambda e, bf=bf, h=h: e.activation(out=g_t[bf][:], in_=e_t[bf][:], func=AF.Ln, scale=lb[:, h:h + 1], bias=1.0),
                         reads=[('e_t', bf), 'lb'], writes=[('g_t', bf)])
                    S.op('dve', lambda e, bf=bf: e.tensor_tensor(out=g_t[bf][:], in0=g_t[bf][:], in1=L1[bf][:], op=ALU.subtract),
                         reads=[('g_t', bf), ('L1', bf)], writes=[('g_t', bf)])
                    for tl in range(4):
                        S.op('dve', lambda e, tl=tl, bf=bf: e.tensor_tensor_scan(out=b_t[bf][:, tl * P:(tl + 1) * P], data0=ones[:],
                                                                            data1=g_t[bf][:, tl * P:(tl + 1) * P], initial=0.0,
                                                                            op0=ALU.mult, op1=ALU.add),
                             reads=[('g_t', bf), 'ones'], writes=[('b_t', bf, tl)])
                    bk = [('b_t', bf, tl) for tl in range(4)]
                    S.op('dve', lambda e, bf=bf: e.tensor_tensor(out=t_t[bf][:], in0=t_t[bf][:], in1=L1[bf][:], op=ALU.subtract),
                         reads=[('t_t', bf), ('L1', bf)], writes=[('t_t', bf)])
                    S.op('dve', lambda e, bf=bf: e.tensor_tensor(out=t_t[bf][:], in0=t_t[bf][:], in1=b_t[bf][:], op=ALU.subtract),
                         reads=[('t_t', bf)] + bk, writes=[('t_t', bf)])
                    S.op('act', lambda e, bf=bf, hb=hb, h=h: e.activation(out=ktT[hb][:], in_=t_t[bf][:], func=AF.Exp, bias=ln1mlb[:, h:h + 1]),
                         reads=[('t_t', bf), 'ln1mlb'], writes=[('ktT', hb)])
                    blast = b_t[bf].rearrange("p (t c) -> p t c", t=4)[:, :, P - 1]
                    S.op('act', lambda e, bf=bf, hb=hb, blast=blast: e.activation(out=decs[hb][:], in_=blast, func=AF.Exp),
                         reads=bk, writes=[('decs', hb)])

                def back(idx, j, h):
                    bf = idx % 2
                    hb = idx % 6
                    jb = j % 2
                    for tl in range(4):
                        tb = tl % 2
                        kt_ps = ps[4 + tb].bitcast(BF16)[:, 0:P]
                        S.op('pe', lambda e, tl=tl, hb=hb, kt_ps=kt_ps: e.transpose(out=kt_ps, in_=ktT[hb][:, tl * P:(tl + 1) * P], identity=ident[:]),
                             reads=[('ktT', hb), 'ident'], writes=[PK(4 + tb)])
                        if tb == 0:
                            S.op('dve', lambda e, tb=tb, kt_ps=kt_ps: e.tensor_copy(out=ktok[tb][:], in_=kt_ps), reads=[PK(4 + tb)], writes=[('ktok', tb)])
                        else:
                            S.op('act', lambda e, tb=tb, kt_ps=kt_ps: e.activation(out=ktok[tb][:], in_=kt_ps, func=AF.Copy), reads=[PK(4 + tb)], writes=[('ktok', tb)])
                        S.op('pe', lambda e, tl=tl, tb=tb, jb=jb, h=h: e.matmul(ps[6 + tb][:, 0:P], lhsT=ktok[tb][:], rhs=vall[jb][:, tl, h * P:(h + 1) * P],
                                                                          start=True, stop=True),
                             reads=[('ktok', tb), ('vall', jb, tl, h // 4)], writes=[PK(6 + tb)])
                        S.op('act', lambda e, tl=tl, tb=tb, hb=hb: e.activation(out=tmpS[tb][:], in_=ps[6 + tb][:, 0:P], func=AF.Copy,
                                                                          scale=decs[hb][:, tl:tl + 1]),
                             reads=[PK(6 + tb), ('decs', hb)], writes=[('tmpS', tb)])
                        S.op('dve', lambda e, tl=tl, tb=tb, hb=hb, h=h: e.scalar_tensor_tensor(out=S_in[:, h, :], in0=S_in[:, h, :], scalar=decs[hb][:, tl:tl + 1],
                                                                                     in1=tmpS[tb][:], op0=ALU.mult, op1=ALU.add),
                             reads=[('tmpS', tb), ('decs', hb), ('S_in', h)], writes=[('S_in', h)])

                seq = [(j, h) for j in range(NB) for h in range(8)]
                vproj(0)
                front(0, *seq[0])
                for idx, (j, h) in enumerate(seq):
                    if idx + 1 < len(seq):
                        nj, nh = seq[idx + 1]
                        if nh == 0:
                            vproj(nj)
                        front(idx + 1, nj, nh)
                    back(idx, j, h)
            if ctx is None:
                S.barrier()

        with contextlib.ExitStack() as pre:
            stg1 = sb("stg1", [P, KC, 512], F32, pre)
            Wf = sb("Wf", [P, KC, D], BF16, pre)
            Wi = sb("Wi", [P, KC, D], BF16, pre)
            for hh in range(2):
                load_w(stg1[:], Wf[:, :, hh * 512:(hh + 1) * 512], w_in_v[:, :, OFF['hf'] + hh * 512:OFF['hf'] + (hh + 1) * 512],
                       ('Wf', hh), 'stg1')
                load_w(stg1[:], Wi[:, :, hh * 512:(hh + 1) * 512], w_in_v[:, :, OFF['hi'] + hh * 512:OFF['hi'] + (hh + 1) * 512],
                       ('Wi', hh), 'stg1')
            actx = stage_a_alloc(pre)
            sctx = summary_alloc(pre)
            for k in range(NPREV):
                stage_a([(1 + i, x_prev[(k * NT + i) * P:(k * NT + i + 1) * P, :]) for i in range(NT)], actx)
                summary_pass(Wf, Wi, sctx)
            stage_a([(0, x_halo)] + [(1 + i, x_own[i * P:(i + 1) * P, :]) for i in range(NT)], actx)
        S.barrier()
        uhT = sb("uhT", [P, KC, T], BF16)
        uaT = sb("uaT", [P, KC, T], BF16)

        with contextlib.ExitStack() as ph:
            stg = sb("stgA", [P, KC, 704], F32, ph)
            Wa = [sb("Wa%d" % i, [P, KC, 704], BF16, ph) for i in range(2)]
            kT = sb("kT", [P, TA], BF16, ph)
            qT = sb("qT", [P, 2, 2, T], BF16, ph)
            S.op('pool', lambda e: e.memset(qT[:], 0.0), writes=['qTz'])
            vext = sb("vext", [P, NT + 1, 65], BF16, ph)
            ga = sb("ga", [P, NT, 256], BF16, ph)
            gtmp = sb("gtmp", [P, 256], F32, ph)
            Pm = [sb("Pm%d" % i, [P, 512], BF16, ph) for i in range(2)]
            den = sb("den", [P, 4], F32, ph)
            uat = sb("uat", [P, 256], BF16, ph)

            def load_grp(g):
                wb = Wa[g % 2]
                parts = [(0, 256, OFF['aq'] + g * 256, 0), (256, 64, OFF['ak'] + g * 64, 1), (320, 64, OFF['ak'] + g * 64, 2),
                         (384, 64, OFF['av'] + g * 64, 3), (448, 256, OFF['ag'] + g * 256, 4)]
                for (d0, n, c0, pi) in parts:
                    load_w(stg[:, :, d0:d0 + n], wb[:, :, d0:d0 + n], w_in_v[:, :, c0:c0 + n], ('Wa', g % 2, pi), ('stgA', pi))

            load_grp(0)
            for g in range(4):
                if g + 1 < 4:
                    load_grp(g + 1)
                wb = Wa[g % 2]
                wk = [('Wa', g % 2, pi) for pi in range(5)]
                S.op('dve', lambda e: e.memset(vext[:, :, 64:65], 1.0), writes=[('vext1',)])
                c0 = 0
                while c0 < TA:
                    n = min(512, TA - c0)
                    for kc in range(KC):
                        S.op('pe', lambda e, kc=kc, c0=c0, n=n, wb=wb: e.matmul(ps[0][:, 0:n], lhsT=wb[:, kc, 256:384],
                                                                             rhs=xnT[:, kc, c0:c0 + n], start=(kc == 0), stop=(kc == KC - 1)),
                             reads=xkb(c0 // P, (c0 + n) // P) + [wk[1], wk[2]], writes=[PK(0)])
                    S.op('act', lambda e, c0=c0, n=n: e.activation(out=kT[:, c0:c0 + n], in_=ps[0][:, 0:n], func=AF.Copy),
                         reads=[PK(0)], writes=[('kT', c0 // 512)])
                    c0 += n
                for j in range(NB):
                    tok0 = P + j * 512
                    for ch in range(2):
                        for kc in range(KC):
                            S.op('pe', lambda e, kc=kc, ch=ch, tok0=tok0, wb=wb: e.matmul(ps[1][:], lhsT=wb[:, kc, ch * P:(ch + 1) * P],
                                                                                       rhs=xnT[:, kc, tok0:tok0 + 512],
                                                                                       start=(kc == 0), stop=(kc == KC - 1)),
                                 reads=xkb(1 + 4 * j, 5 + 4 * j) + [wk[0]], writes=[PK(1)])
                        S.op('dve', lambda e, ch=ch, j=j: e.tensor_copy(out=qT[0:64, 0, ch, j * 512:(j + 1) * 512], in_=ps[1][0:64, :]),
                             reads=[PK(1), 'qTz'], writes=[('qT', j, ch, 0)])
                        S.op('act', lambda e, ch=ch, j=j: e.activation(out=qT[64:128, 1, ch, j * 512:(j + 1) * 512], in_=ps[1][64:128, :],
                                                                   func=AF.Copy),
                             reads=[PK(1), 'qTz'], writes=[('qT', j, ch, 1)])
                for ti in range(NT + 1):
                    for kc in range(KC):
                        S.op('pe', lambda e, kc=kc, ti=ti, wb=wb: e.matmul(ps[2][:, 0:64], lhsT=xnT[:, kc, ti * P:(ti + 1) * P],
                                                                        rhs=wb[:, kc, 384:448], start=(kc == 0), stop=(kc == KC - 1)),
                             reads=xk(ti) + [wk[3]], writes=[PK(2)])
                    S.op('act', lambda e, ti=ti: e.activation(out=vext[:, ti, 0:64], in_=ps[2][:, 0:64], func=AF.Copy),
                         reads=[PK(2)], writes=[('vext', ti)])
                    if ti >= 1:
                        for kc in range(KC):
                            S.op('pe', lambda e, kc=kc, ti=ti, wb=wb: e.matmul(ps[3][:, 0:256], lhsT=xnT[:, kc, ti * P:(ti + 1) * P],
                                                                            rhs=wb[:, kc, 448:704], start=(kc == 0), stop=(kc == KC - 1)),
                                 reads=xk(ti) + [wk[4]], writes=[PK(3)])
                        S.op('act', lambda e: e.activation(out=gtmp[:], in_=ps[3][:, 0:256], func=AF.Exp, scale=-1.0),
                             reads=[PK(3)], writes=['gtmp'])
                        S.op('act', lambda e: e.activation(out=gtmp[:], in_=gtmp[:], func=AF.Ln, bias=1.0), reads=['gtmp'], writes=['gtmp'])
                        S.op('act', lambda e: e.activation(out=gtmp[:], in_=gtmp[:], func=AF.Exp, scale=-1.0), reads=['gtmp'], writes=['gtmp'])
                        S.op('dve', lambda e, ti=ti: e.tensor_tensor(out=ga[:, ti - 1, :], in0=ps[3][:, 0:256], in1=gtmp[:], op=ALU.mult),
                             reads=[PK(3), 'gtmp'], writes=[('ga', ti - 1)])
                for ti in range(NT):
                    q_r = [('qT', ti // 4, c_, h_) for c_ in range(2) for h_ in range(2)]
                    for kb in range(2):
                        kcol = (ti + kb) * P
                        bank = 4 + kb
                        for hp in range(2):
                            o_ap = ps[bank][:, hp * 256:(hp + 1) * 256]
                            S.op('pe', lambda e, hp=hp, kcol=kcol, o_ap=o_ap, ti=ti: e.matmul(
                                o_ap, lhsT=kT[:, kcol:kcol + P],
                                rhs=qT[:, hp, :, ti * P:(ti + 1) * P], start=True, stop=True),
                                 reads=q_r + [('kT', kcol // 512)], writes=[PK(bank)])
                        S.op('act', lambda e, kb=kb, bank=bank: e.activation(out=Pm[kb][:], in_=ps[bank][:], func=AF.Exp, scale=0.125),
                             reads=[PK(bank)], writes=[('Pm', kb)])
                        msk = mC if kb == 1 else (mP0 if ti == 0 else mP)
                        S.op('dve', lambda e, kb=kb, msk=msk: e.tensor_tensor(out=Pm[kb][:], in0=Pm[kb][:], in1=msk[:], op=ALU.mult),
                             reads=[('Pm', kb), 'mC', 'mP', 'mP0'], writes=[('Pm', kb)])
                    pso = ps[6].rearrange("p (a t) -> p a t", a=4)
                    for a in range(4):
                        for kb in range(2):
                            S.op('pe', lambda e, a=a, kb=kb, ti=ti, pso=pso: e.matmul(pso[:, a, 0:65], lhsT=Pm[kb][:, ((a % 2) * 2 + a // 2) * P:((a % 2) * 2 + a // 2 + 1) * P],
                                                                                   rhs=vext[:, ti + kb, :], start=(kb == 0), stop=(kb == 1)),
                                 reads=[('Pm', kb), ('vext', ti + kb), ('vext1',)], writes=[PK(6)])
                    S.op('dve', lambda e, pso=pso, g=g: e.tensor_tensor(out=den[:], in0=pso[:, :, 64], in1=esink[:, 4 * g:4 * g + 4], op=ALU.add),
                         reads=[PK(6), 'esink'], writes=['den'])
                    S.op('dve', lambda e: e.reciprocal(out=den[:], in_=den[:]), reads=['den'], writes=['den'])
                    for a in range(4):
                        S.op('dve', lambda e, a=a, pso=pso, ti=ti: e.scalar_tensor_tensor(
                            out=uat[:, a * 64:(a + 1) * 64], in0=pso[:, a, 0:64], scalar=den[:, a:a + 1],
                            in1=ga[:, ti, a * 64:(a + 1) * 64], op0=ALU.mult, op1=ALU.mult),
                             reads=[PK(6), 'den', ('ga', ti)], writes=[('uat', a)])
                    ut_ps = ps[7].bitcast(BF16).rearrange("p (c t) -> p c t", c=8)
                    for cc in range(2):
                        S.op('pe', lambda e, cc=cc, ut_ps=ut_ps: e.transpose(out=ut_ps[:, cc, :], in_=uat[:, cc * P:(cc + 1) * P], identity=ident[:]),
                             reads=[('uat', a) for a in range(4)] + ['ident'], writes=[PK(7)])
                    S.op('act', lambda e, ut_ps=ut_ps, g=g, ti=ti: e.activation(out=uaT[:, 2 * g:2 * g + 2, ti * P:(ti + 1) * P],
                                                                           in_=ut_ps[:, 0:2, :], func=AF.Copy),
                         reads=[PK(7)], writes=[('uaT', g, ti)])
        S.barrier()
        hgrn_phase(full=True)

        with contextlib.ExitStack() as ph:
            stg = sb("stgP", [P, KC, P], F32, ph)
            Wc = [sb("Wc%d" % i, [P, KC, 512], BF16, ph) for i in range(2)]
            Wo = sb("Wo", [P, KC, D], BF16, ph)
            mT = sb("mT", [P, KC, T], BF16, ph)
            sgL = [[sb("sg%d_%d" % (q_, i), [P, 512], F32, ph) for i in range(2)] for q_ in range(2)]
            m1L = [sb("m1_%d" % q_, [P, 512], F32, ph) for q_ in range(2)]
            xt = [sb("xtP%d" % i, [P, D], F32, ph) for i in range(2)]
            junk = sb("junkP", [P, D], BF16, ph)
            ss2 = sb("ss2", [P, 1], F32, ph)
            rs2 = sb("rs2", [P, 1], F32, ph)
            fnw = sb("fnw", [P, D], F32, ph)
            S.dma(fnw[:], fnw_d, writes=['fnw'])
            w_bh_v = w_bh.rearrange("(kc p) n -> p kc n", p=P)
            w_ba_v = w_ba.rearrange("(kc p) n -> p kc n", p=P)
            w_out_v = w_out.rearrange("(kc p) n -> p kc n", p=P)

            def load_chunk(mc):
                wb = Wc[mc % 2]
                srcs = [w_bh_v[:, :, mc * P:(mc + 1) * P], w_ba_v[:, :, mc * P:(mc + 1) * P],
                        w_in_v[:, :, OFF['mh'] + mc * P:OFF['mh'] + (mc + 1) * P],
                        w_in_v[:, :, OFF['ma'] + mc * P:OFF['ma'] + (mc + 1) * P]]
                for pi, s_ap in enumerate(srcs):
                    load_w(stg[:], wb[:, :, pi * P:(pi + 1) * P], s_ap, ('Wc', mc % 2, pi), 'stgP')

            def post_item(mc, j, wb, wk):
                par = (mc * NB + j) % 2
                pb = 4 * par
                sg = sgL[par]
                m1 = m1L[par]
                tok0 = P + j * 512
                rk = xkb(1 + 4 * j, 5 + 4 * j)
                for kc in range(KC):
                    S.op('pe', lambda e, kc=kc, wb=wb, j=j: e.matmul(ps[pb][:], lhsT=wb[:, kc, 0:P],
                                                                  rhs=uhT[:, kc, j * 512:(j + 1) * 512], start=(kc == 0), stop=(kc == KC - 1)),
                         reads=[wk[0]], writes=[PK(pb)])
                for kc in range(KC):
                    S.op('pe', lambda e, kc=kc, wb=wb, j=j: e.matmul(ps[pb + 1][:], lhsT=wb[:, kc, P:2 * P],
                                                                  rhs=uaT[:, kc, j * 512:(j + 1) * 512], start=(kc == 0), stop=(kc == KC - 1)),
                         reads=[wk[1]], writes=[PK(pb + 1)])
                for w in range(2):
                    for kc in range(KC):
                        S.op('pe', lambda e, kc=kc, wb=wb, w=w, tok0=tok0: e.matmul(
                            ps[pb + 2 + w][:], lhsT=wb[:, kc, (2 + w) * P:(3 + w) * P],
                            rhs=xnT[:, kc, tok0:tok0 + 512], start=(kc == 0), stop=(kc == KC - 1)),
                             reads=rk + [wk[2 + w]], writes=[PK(pb + 2 + w)])
                    S.op('act', lambda e, w=w: e.activation(out=sg[w][:], in_=ps[pb + 2 + w][:], func=AF.Exp, scale=-1.0),
                         reads=[PK(pb + 2 + w)], writes=[('sg', par, w)])
                    S.op('act', lambda e, w=w: e.activation(out=sg[w][:], in_=sg[w][:], func=AF.Ln, bias=1.0),
                         reads=[('sg', par, w)], writes=[('sg', par, w)])
                    S.op('act', lambda e, w=w: e.activation(out=sg[w][:], in_=sg[w][:], func=AF.Exp, scale=-1.0),
                         reads=[('sg', par, w)], writes=[('sg', par, w)])
                S.op('dve', lambda e: e.tensor_tensor(out=m1[:], in0=ps[pb][:], in1=sg[0][:], op=ALU.mult),
                     reads=[PK(pb), ('sg', par, 0)], writes=[('m1', par)])
                S.op('dve', lambda e: e.tensor_tensor(out=sg[1][:], in0=ps[pb + 1][:], in1=sg[1][:], op=ALU.mult),
                     reads=[PK(pb + 1), ('sg', par, 1)], writes=[('sg', par, 1)])
                S.op('dve', lambda e, mc=mc, j=j: e.tensor_tensor(out=mT[:, mc, j * 512:(j + 1) * 512], in0=m1[:], in1=sg[1][:], op=ALU.add),
                     reads=[('m1', par), ('sg', par, 1)], writes=[('mT', mc, j)])

            load_chunk(0)
            for mc in range(KC):
                if mc + 1 < KC:
                    load_chunk(mc + 1)
                wb = Wc[mc % 2]
                wk = [('Wc', mc % 2, pi) for pi in range(4)]
                for j in range(NB):
                    post_item(mc, j, wb, wk)

            for q in range(KC):
                load_w(stg[:], Wo[:, :, q * P:(q + 1) * P], w_out_v[:, :, q * P:(q + 1) * P], ('Wo', q // 4), 'stgP')
            for ti in range(NT):
                j = ti // 4
                b = ti % 2
                S.dma(xt[b][:], x_own[ti * P:(ti + 1) * P, :], writes=[('xtP', b)])
                for nh in range(2):
                    bank = 4 + (2 * ti + nh) % 4
                    for mc in range(KC):
                        S.op('pe', lambda e, mc=mc, nh=nh, ti=ti, bank=bank: e.matmul(
                            ps[bank][:], lhsT=mT[:, mc, ti * P:(ti + 1) * P], rhs=Wo[:, mc, nh * 512:(nh + 1) * 512],
                            start=(mc == 0), stop=(mc == KC - 1)),
                             reads=[('mT', m, j) for m in range(KC)] + [('Wo', nh)], writes=[PK(bank)])
                    S.op('dve', lambda e, nh=nh, b=b, bank=bank: e.tensor_tensor(out=xt[b][:, nh * 512:(nh + 1) * 512], in0=ps[bank][:],
                                                                            in1=xt[b][:, nh * 512:(nh + 1) * 512], op=ALU.add),
                         reads=[PK(bank), ('xtP', b)], writes=[('xtP', b)])
                S.op('act', lambda e, b=b: e.activation(out=junk[:], in_=xt[b][:], func=AF.Square, accum_out=ss2[:]),
                     reads=[('xtP', b)], writes=['junkP', 'ss2'])
                S.op('act', lambda e: e.activation(out=rs2[:], in_=ss2[:], func=AF.Ln, scale=1.0 / D, bias=epsb[:]),
                     reads=['ss2', 'epsb'], writes=['rs2'])
                S.op('act', lambda e: e.activation(out=rs2[:], in_=rs2[:], func=AF.Exp, scale=-0.5), reads=['rs2'], writes=['rs2'])
                S.op('dve', lambda e, b=b: e.scalar_tensor_tensor(out=xt[b][:], in0=xt[b][:], scalar=rs2[:, 0:1], in1=fnw[:],
                                                              op0=ALU.mult, op1=ALU.mult),
                     reads=[('xtP', b), 'rs2', 'fnw'], writes=[('xtP', b)])
                S.dma(out_d[ti * P:(ti + 1) * P, :], xt[b][:], reads=[('xtP', b)], writes=[('outst', ti)])
        S.finish('sp')
        S.barrier()
    return nc


def _host_inputs(x, norm_w, w_in, hgrn_lower_bound, hgrn_norm_w, w_branch_hgrn, attn_sinks,
                 w_branch_attn, w_out, final_norm_w, NT):
    f32 = np.float32
    bf = ml_dtypes.bfloat16
    x = np.asarray(x, f32)
    B, SEQ, _ = x.shape
    T = NT * P
    nseg = SEQ // T
    w_in0 = np.ascontiguousarray(np.asarray(w_in, f32)[0])
    wbh = np.ascontiguousarray(np.asarray(w_branch_hgrn, f32)[0])
    wba = np.ascontiguousarray(np.asarray(w_branch_attn, f32)[0])
    wo = np.ascontiguousarray(np.asarray(w_out, f32)[0])
    nw = np.ascontiguousarray(np.asarray(norm_w, f32)[0].reshape(KC, P).T)
    lbp = np.asarray(hgrn_lower_bound, f32)
    lbp = np.ascontiguousarray(np.concatenate([lbp[0].reshape(8, P).T, lbp[1].reshape(8, P).T], axis=1))
    hnw = np.ascontiguousarray(np.asarray(hgrn_norm_w, f32)[0].reshape(8, P).T)
    sinks = np.ascontiguousarray(np.broadcast_to(np.asarray(attn_sinks, f32)[0][None, :], (P, 16)))
    fnw = np.ascontiguousarray(np.broadcast_to(np.asarray(final_norm_w, f32)[None, :], (P, D)))
    ident = np.eye(P, dtype=f32).astype(bf)
    si = np.arange(P)[:, None]
    tj = np.arange(P)[None, :]
    mC = np.tile((si <= tj).astype(f32), (1, 4)).astype(bf)
    mPm = np.tile((si > tj).astype(f32), (1, 4)).astype(bf)
    zeros_m = np.zeros((P, 512), f32).astype(bf)
    in_maps = []
    ncores = B * nseg
    for c in range(ncores):
        b, s = divmod(c, nseg)
        xo = np.ascontiguousarray(x[b, s * T:(s + 1) * T, :])
        xh = np.ascontiguousarray(x[b, s * T - P:s * T, :]) if s > 0 else np.zeros((P, D), f32)
        xp = np.zeros((NPREV * T, D), f32)
        for k in range(NPREV):
            sp = s - NPREV + k
            if sp >= 0:
                xp[k * T:(k + 1) * T] = x[b, sp * T:(sp + 1) * T, :]
        in_maps.append(dict(x_own=xo, x_halo=xh, w_in=w_in0, w_bh=wbh, w_ba=wba, w_out=wo, nw=nw, lbp=lbp, hnw=hnw,
                            sinks=sinks, fnw=fnw, x_prev=xp, ident=ident, maskC=mC, maskP=mPm,
                            maskP0=(zeros_m if s == 0 else mPm)))
    return in_maps, B, nseg, T


def kernel(x, norm_w, w_in, hgrn_lower_bound, hgrn_norm_w, w_branch_hgrn, attn_sinks,
           w_branch_attn, w_out, final_norm_w):
    NT = 16
    in_maps, B, nseg, T = _host_inputs(x, norm_w, w_in, hgrn_lower_bound, hgrn_norm_w, w_branch_hgrn, attn_sinks,
                                       w_branch_attn, w_out, final_norm_w, NT)
    nc = build_nc(NT)
    res = run_bass_kernel_spmd(nc, in_maps, core_ids=list(range(NCORES)))
    out = np.zeros((B, nseg * T, D), np.float32)
    for c in range(NCORES):
        b, s = divmod(c, nseg)
        out[b, s * T:(s + 1) * T, :] = np.asarray(res.results[c]["out"], np.float32)
    return out
```

```python
import contextlib
import numpy as np
import ml_dtypes
import concourse.bass as bass
import concourse.mybir as mybir
from concourse.bass_utils import run_bass_kernel_spmd

F32 = mybir.dt.float32
BF16 = mybir.dt.bfloat16
AF = mybir.ActivationFunctionType
ALU = mybir.AluOpType

P = 128
D = 1024
KC = 8
DIN = 8704
NCORES = 8
SEG_PER_BATCH = 4
OFF = dict(hq=0, hf=1024, hi=2048, hg=3072, aq=4096, ak=5120, av=5376, ag=5632, mh=6656, ma=7680)
EPS = 1e-6
NPREV = 3


class Sched:
    ENG = ['pe', 'act', 'dve', 'pool', 'sp']
    EPOCH = 1500

    def __init__(self, nc, es, n_dma_sems=12):
        self.nc = nc
        self.es = es
        self.streams = {e: [] for e in self.ENG}
        self.sem = {e: es.enter_context(nc.semaphore('c_' + e)) for e in self.ENG}
        self.cnt = {e: 0 for e in self.ENG}
        self.waited = {e: {} for e in self.ENG}
        self.res = {}
        self.dsem = [es.enter_context(nc.semaphore('d%d' % i)) for i in range(n_dma_sems)]
        self.dval = [0] * n_dma_sems
        self.dnext = 0
        self.semname = {}
        for e in self.ENG:
            self.semname[id(self.sem[e])] = self.sem[e]
        for s in self.dsem:
            self.semname[id(s)] = s
        self.all_events = {}
        self.skip_sids = set()
        self.nops = 0
        self.pending = []
        self.engobj = dict(pe=nc.tensor, act=nc.scalar, dve=nc.vector, pool=nc.gpsimd, sp=nc.sync)

    def _wait(self, eng, ev):
        sid, val = ev
        if self.waited[eng].get(sid, 0) >= val:
            return
        self.waited[eng][sid] = val
        sem = self.semname[sid]
        self.engobj[eng].wait_ge(sem, val)

    def _deps(self, eng, reads, writes):
        deps = []
        own = id(self.sem[eng])
        for r in reads:
            st = self.res.get(r)
            if st and st['w']:
                deps.append(st['w'])
        for w in writes:
            st = self.res.get(w)
            if st:
                if st['w']:
                    deps.append(st['w'])
                for sid, val in st['r'].items():
                    deps.append((sid, val))
        for ev in deps:
            if eng == 'pe' and ev[0] == own:
                continue
            self._wait(eng, ev)

    def _mark(self, ev, reads, writes):
        for r in reads:
            st = self.res.setdefault(r, {'w': None, 'r': {}})
            st['r'][ev[0]] = max(st['r'].get(ev[0], 0), ev[1])
        for w in writes:
            self.res[w] = {'w': ev, 'r': {}}
        self.all_events[ev[0]] = max(self.all_events.get(ev[0], 0), ev[1])

    COST = dict(pe=0.12, act=0.45, dve=0.40, pool=1.0, sp=1.5)
    LAT = 0.4
    SLACK = 0.15

    class _Probe:
        def __init__(self):
            self.out = None

        def __getattr__(self, name):
            def rec(*a, **k):
                o = k.get('out', a[0] if a else None)
                if o is None:
                    o = k.get('ap')
                self.out = o
                return self
            return rec

    def _est(self, eng, fn):
        try:
            pr = Sched._Probe()
            fn(pr)
            shp = tuple(pr.out.shape)
            n = 1
            for s_ in shp[1:]:
                n *= int(s_)
        except Exception:
            return self.COST[eng]
        if eng == 'pe':
            return 0.06 + n / 2400.0
        if eng == 'act':
            return 0.20 + n * 0.0006
        if eng == 'dve':
            return 0.20 + n * 0.0008
        if eng == 'pool':
            return 0.3 + n * 0.003
        return self.COST[eng]

    def op(self, eng, fn, reads=(), writes=(), cost=None):
        self.pending.append(('op', eng, fn, list(reads), list(writes), self._est(eng, fn)))

    def dma(self, out, in_, reads=(), writes=(), eng='sp'):
        try:
            nel = 1
            for s_ in tuple(out.shape):
                nel *= int(s_)
            cost = 2.0 + nel * 4 / 250e3
        except Exception:
            cost = self.COST['sp']
        self.pending.append(('dma', eng, (out, in_), list(reads), list(writes), cost))

    def flush(self):
        ops = self.pending
        self.pending = []
        n = len(ops)
        if n == 0:
            return
        lastw = {}
        readers = {}
        preds = [set() for _ in range(n)]
        for i, (kind, eng, fn, reads, writes, cost) in enumerate(ops):
            pk = [r for r in reads if isinstance(r, tuple) and r and r[0] == 'ps']
            rd = [r for r in reads if r not in pk]
            wr = list(writes) + [r for r in pk if r not in writes]
            for r in rd:
                if r in lastw:
                    preds[i].add(lastw[r])
            for w in wr:
                if w in lastw:
                    preds[i].add(lastw[w])
                for j in readers.get(w, ()):
                    preds[i].add(j)
            for r in rd:
                readers.setdefault(r, []).append(i)
            for w in wr:
                lastw[w] = i
                readers[w] = []
            preds[i].discard(i)
        succs = [[] for _ in range(n)]
        npred = [0] * n
        for i in range(n):
            npred[i] = len(preds[i])
            for j in preds[i]:
                succs[j].append(i)
        fin = [0.0] * n
        ready_t = [0.0] * n
        avail = {e: 0.0 for e in self.ENG}
        ready = [i for i in range(n) if npred[i] == 0]
        order = []
        last_eng_idx = {e: -1 for e in self.ENG}
        rank = [0.0] * n
        for i in range(n - 1, -1, -1):
            m_ = 0.0
            for s in succs[i]:
                if rank[s] > m_:
                    m_ = rank[s]
            rank[i] = ops[i][5] + self.LAT + m_
        while ready:
            sts = {}
            mn = None
            for i in ready:
                st = max(ready_t[i], avail[ops[i][1]])
                sts[i] = st
                if mn is None or st < mn:
                    mn = st
            best = None
            for i in ready:
                if sts[i] <= mn + self.SLACK:
                    key = (-rank[i], i)
                    if best is None or key < best[0]:
                        best = (key, i)
            i = best[1]
            st = sts[i]
            ready.remove(i)
            eng = ops[i][1]
            fin[i] = st + ops[i][5]
            avail[eng] = fin[i]
            order.append(i)
            for s in succs[i]:
                lat = 0.0 if (ops[s][1] == eng == 'pe') else self.LAT
                ready_t[s] = max(ready_t[s], fin[i] + lat)
                npred[s] -= 1
                if npred[s] == 0:
                    ready.append(s)
        assert len(order) == n
        for i in order:
            kind, eng, fn, reads, writes, cost = ops[i]
            if kind == 'op':
                self._op_now(eng, fn, reads, writes)
            else:
                self._dma_now(fn[0], fn[1], reads, writes, eng)

    def _op_now(self, eng, fn, reads=(), writes=()):
        self.nops += 1
        pk = [r for r in reads if isinstance(r, tuple) and r and r[0] == 'ps']
        if pk:
            reads = [r for r in reads if r not in pk]
            writes = list(writes) + [r for r in pk if r not in writes]
        self._deps(eng, reads, writes)
        if self.cnt[eng] >= self.EPOCH:
            ns = self.es.enter_context(self.nc.semaphore('c_%s_%d' % (eng, len(self.semname))))
            self.semname[id(ns)] = ns
            self.sem[eng] = ns
            self.cnt[eng] = 0
        self.cnt[eng] += 1
        sem = self.sem[eng]
        ev = (id(sem), self.cnt[eng])
        fn(self.engobj[eng]).then_inc(sem, 1)
        self._mark(ev, reads, writes)
        return ev

    def _dma_now(self, out, in_, reads=(), writes=(), eng='sp'):
        k = self.dnext
        self.dnext = (self.dnext + 1) % len(self.dsem)
        sem = self.dsem[k]
        if self.dval[k] > 0:
            self._wait(eng, (id(sem), self.dval[k]))
        self._deps(eng, reads, writes)
        self.dval[k] += 16
        ev = (id(sem), self.dval[k])
        self.engobj[eng].dma_start(out=out, in_=in_).then_inc(sem, 16)
        self._mark(ev, reads, writes)
        return ev

    def custom(self, eng, fn, sem, val, reads=(), writes=()):
        self._deps(eng, reads, writes)
        self.semname[id(sem)] = sem
        self.skip_sids.add(id(sem))
        ev = (id(sem), val)
        fn(self.engobj[eng])
        self._mark(ev, reads, writes)
        return ev

    def barrier(self):
        self.flush()
        for e in self.ENG:
            for sid, val in list(self.all_events.items()):
                if sid in self.skip_sids:
                    continue
                self._wait(e, (sid, val))
        keep = {k: {'w': v['w'], 'r': {}} for k, v in self.res.items() if v['w'] and v['w'][0] in self.skip_sids}
        self.res = keep

    def finish(self, eng='sp'):
        self.flush()
        for sid, val in list(self.all_events.items()):
            self._wait(eng, (sid, val))

    def emit(self):
        nc = self.nc
        with nc.Block() as block:
            @block.tensor
            def _(e):
                for f in self.streams['pe']:
                    f(e)

            @block.scalar
            def _(e):
                for f in self.streams['act']:
                    f(e)

            @block.vector
            def _(e):
                for f in self.streams['dve']:
                    f(e)

            @block.gpsimd
            def _(e):
                for f in self.streams['pool']:
                    f(e)

            @block.sync
            def _(e):
                for f in self.streams['sp']:
                    f(e)


def build_nc(NT, stop=None, use_cc=True, limit=None, cc_first=False, pad_kb=0):
    T = NT * P
    NB = NT // 4
    TA = (NT + 1) * P
    nc = bass.Bass("TRN2", target_bir_lowering=False)

    def din(name, shape, dt=F32):
        return nc.dram_tensor(name, list(shape), dt, kind="ExternalInput").ap()

    x_own = din("x_own", [T, D])
    x_halo = din("x_halo", [P, D])
    w_in = din("w_in", [D, DIN])
    w_bh = din("w_bh", [D, D])
    w_ba = din("w_ba", [D, D])
    w_out = din("w_out", [D, D])
    nw_d = din("nw", [P, KC])
    lbp_d = din("lbp", [P, 16])
    hnw_d = din("hnw", [P, 8])
    sink_d = din("sinks", [P, 16])
    fnw_d = din("fnw", [P, D])
    ident_d = din("ident", [P, P], BF16)
    mC_d = din("maskC", [P, 512], BF16)
    mP_d = din("maskP", [P, 512], BF16)
    mP0_d = din("maskP0", [P, 512], BF16)
    out_d = nc.dram_tensor("out", [T, D], F32, kind="ExternalOutput").ap()
    x_prev = din("x_prev", [NPREV * T, D])

    w_in_v = w_in.rearrange("(kc p) n -> p kc n", p=P)

    with contextlib.ExitStack() as es:
        S = Sched(nc, es)
        build_nc.last_sched = S

        uniq = [0]

        def sb(name, shape, dt=F32, stack=None):
            uniq[0] += 1
            return (stack or es).enter_context(nc.sbuf_tensor("sb%d_%s" % (uniq[0], name), list(shape), dt))

        if pad_kb:
            sb("padding", [P, pad_kb * 256], F32)
        ps = [es.enter_context(nc.psum_tensor("ps%d" % i, [P, 512], F32)) for i in range(8)]

        def PK(i):
            return ('ps', i)

        xnT = sb("xnT", [P, KC, TA], BF16)
        ident = sb("ident", [P, P], BF16)
        mC = sb("mC", [P, 512], BF16)
        mP = sb("mP", [P, 512], BF16)
        mP0 = sb("mP0", [P, 512], BF16)
        nw = sb("nw", [P, KC])
        lbp = sb("lbp", [P, 16])
        hnw = sb("hnw", [P, 8])
        esink = sb("esink", [P, 16])
        lb = sb("lb", [P, 8])
        ln1mlb = sb("ln1mlb", [P, 8])
        tmp8 = sb("tmp8", [P, 8])
        ones = sb("ones", [P, P])
        epsb = sb("epsb", [P, 1])
        S_in = sb("S_in", [P, 8, P])

        def xk(ti):
            return [('xnT', ti, c) for c in range(KC)]

        def xkb(t0, t1):
            r = []
            for ti in range(t0, t1):
                r += xk(ti)
            return r

        for (dst, src, key) in [(ident, ident_d, 'ident'), (mC, mC_d, 'mC'), (mP, mP_d, 'mP'), (mP0, mP0_d, 'mP0'),
                                (nw, nw_d, 'nw'), (lbp, lbp_d, 'lbp'), (hnw, hnw_d, 'hnw'), (esink, sink_d, 'esink')]:
            S.dma(dst[:], src, writes=[key])
        S.op('dve', lambda e: e.memset(ones[:], 1.0), writes=['ones'])
        S.op('dve', lambda e: e.memset(epsb[:], EPS), writes=['epsb'])
        S.op('dve', lambda e: e.memset(S_in[:], 0.0), writes=[('S_in', h_) for h_ in range(8)])
        S.op('dve', lambda e: e.tensor_tensor(out=tmp8[:], in0=lbp[:, 8:16], in1=lbp[:, 0:8], op=ALU.subtract),
             reads=['lbp'], writes=['tmp8'])
        S.op('act', lambda e: e.activation(out=tmp8[:], in_=tmp8[:], func=AF.Exp), reads=['tmp8'], writes=['tmp8'])
        S.op('act', lambda e: e.activation(out=tmp8[:], in_=tmp8[:], func=AF.Ln, bias=1.0), reads=['tmp8'], writes=['tmp8'])
        S.op('act', lambda e: e.activation(out=lb[:], in_=tmp8[:], func=AF.Exp, scale=-1.0), reads=['tmp8'], writes=['lb'])
        S.op('act', lambda e: e.activation(out=ln1mlb[:], in_=lb[:], func=AF.Ln, scale=-1.0, bias=1.0),
             reads=['lb'], writes=['ln1mlb'])
        S.op('act', lambda e: e.activation(out=esink[:], in_=esink[:], func=AF.Exp), reads=['esink'], writes=['esink'])

        lw_cnt = [0]

        def load_w(stg, dst_ap, src_ap, key, skey):
            S.dma(stg, src_ap, writes=[skey])
            lw_cnt[0] += 1
            if lw_cnt[0] % 3 == 0:
                S.op('act', lambda e: e.activation(out=dst_ap, in_=stg, func=AF.Copy), reads=[skey], writes=[key], cost=1.5)
            else:
                S.op('pool', lambda e: e.tensor_copy(out=dst_ap, in_=stg), reads=[skey], writes=[key], cost=2.5)

        def stage_a_alloc(ph):
            return dict(xt=[sb("xt%d" % i, [P, D], F32, ph) for i in range(3)],
                        xs=[sb("xs%d" % i, [P, D], BF16, ph) for i in range(2)],
                        junk=sb("junkA", [P, D], BF16, ph),
                        ssq=sb("ssq", [P, NT + 1], F32, ph),
                        lnms=sb("lnms", [P, NT + 1], F32, ph),
                        rstd=sb("rstd", [P, NT + 1], F32, ph))

        def stage_a(tiles, ctx=None):
            with contextlib.ExitStack() as ph:
                A = ctx if ctx is not None else stage_a_alloc(ph)
                xt, xs, junk, ssq, lnms, rstd = A['xt'], A['xs'], A['junk'], A['ssq'], A['lnms'], A['rstd']
                S.op('dve', lambda e: e.memset(ssq[:], 0.0), writes=[('ssq', i) for i in range(NT + 1)])
                for n_, (ti, src_ap) in enumerate(tiles):
                    b = n_ % 2
                    b3 = n_ % 3
                    S.dma(xt[b3][:], src_ap, writes=[('xt', b3)])
                    S.op('act', lambda e, b3=b3, ti=ti: e.activation(out=junk[:], in_=xt[b3][:], func=AF.Square,
                                                                accum_out=ssq[:, ti:ti + 1]),
                         reads=[('xt', b3), ('ssq', ti)], writes=['junkA', ('ssq', ti)], cost=1.1)
                    S.op('act', lambda e, ti=ti: e.activation(out=lnms[:, ti:ti + 1], in_=ssq[:, ti:ti + 1], func=AF.Ln,
                                                          scale=1.0 / D, bias=epsb[:]),
                         reads=[('ssq', ti), 'epsb'], writes=[('lnms', ti)])
                    S.op('act', lambda e, ti=ti: e.activation(out=rstd[:, ti:ti + 1], in_=lnms[:, ti:ti + 1], func=AF.Exp,
                                                          scale=-0.5),
                         reads=[('lnms', ti)], writes=[('rstd', ti)])
                    S.op('dve', lambda e, b=b, b3=b3, ti=ti: e.tensor_scalar(out=xs[b][:], in0=xt[b3][:], scalar1=rstd[:, ti:ti + 1],
                                                                  scalar2=None, op0=ALU.mult),
                         reads=[('xt', b3), ('rstd', ti)], writes=[('xs', b)], cost=0.8)
                    pst = ps[b].bitcast(BF16).rearrange("p (c t) -> p c t", c=KC)
                    for c in range(KC):
                        S.op('pe', lambda e, b=b, c=c, pst=pst: e.transpose(out=pst[:, c, :], in_=xs[b][:, c * P:(c + 1) * P],
                                                                        identity=ident[:]),
                             reads=[('xs', b), 'ident'], writes=[PK(b)])
                    for c in range(KC):
                        dst = xnT[:, c, ti * P:(ti + 1) * P]
                        if c % 2 == 0:
                            S.op('dve', lambda e, c=c, pst=pst, dst=dst: e.tensor_scalar(out=dst, in0=pst[:, c, :],
                                                                                    scalar1=nw[:, c:c + 1], scalar2=None,
                                                                                    op0=ALU.mult),
                                 reads=[PK(b), 'nw'], writes=[('xnT', ti, c)])
                        else:
                            S.op('act', lambda e, c=c, pst=pst, dst=dst: e.activation(out=dst, in_=pst[:, c, :], func=AF.Copy,
                                                                                 scale=nw[:, c:c + 1]),
                                 reads=[PK(b), 'nw'], writes=[('xnT', ti, c)])
            if ctx is None:
                S.barrier()

        def hgrn_phase(full):
            with contextlib.ExitStack() as ph:
                tag = 'F' if full else 'S'
                stg = sb("stg" + tag, [P, KC, 512], F32, ph)
                Wh = [sb("Wh%s%d" % (tag, i), [P, KC, 512], BF16, ph) for i in range(4)]
                e_tL = [sb("e_tF%d" % i_, [P, 512], F32, ph) for i_ in range(2)]
                L1L = [sb("L1F%d" % i_, [P, 512], F32, ph) for i_ in range(2)]
                g_tL = [sb("g_tF%d" % i_, [P, 512], F32, ph) for i_ in range(2)]
                b_tL = [sb("b_tF%d" % i_, [P, 512], F32, ph) for i_ in range(2)]
                t_tL = [sb("t_tF%d" % i_, [P, 512], F32, ph) for i_ in range(2)]
                ktTL = [sb("ktTF%d" % i_, [P, 512], BF16, ph) for i_ in range(2)]
                vbfL = [sb("vbfF%d" % i_, [P, 4, P], BF16, ph) for i_ in range(2)]
                ktokL = [sb("ktokH%d" % i_, [P, P], BF16, ph) for i_ in range(2)]
                S32L = [sb("S32H%d" % i_, [P, P], F32, ph) for i_ in range(2)]
                StmpL = [sb("StmpH%d" % i_, [P, P], F32, ph) for i_ in range(2)]
                decL = [sb("decH%d" % i_, [P, 1], F32, ph) for i_ in range(2)]
                sumb = sb("sumb" + tag, [P, 8], F32, ph)
                if full:
                    eqL = [sb("eqF%d" % i_, [P, 512], F32, ph) for i_ in range(2)]
                    qtTL = [sb("qtTF%d" % i_, [P, 512], BF16, ph) for i_ in range(2)]
                    gateL = [sb("gateF%d" % i_, [P, 4, P], F32, ph) for i_ in range(2)]
                    SbfL = [sb("SbfH%d" % i_, [P, P], BF16, ph) for i_ in range(2)]
                    ATmL = [sb("ATmH%d" % i_, [P, P], BF16, ph) for i_ in range(2)]
                    junkL = [sb("junkHH%d" % i_, [P, P], BF16, ph) for i_ in range(2)]
                    ssoL = [sb("ssoH%d" % i_, [P, 1], F32, ph) for i_ in range(2)]
                    rsoL = [sb("rsoH%d" % i_, [P, 1], F32, ph) for i_ in range(2)]
                    utokL = [sb("utokH%d" % i_, [P, P], BF16, ph) for i_ in range(2)]

                def load_head(h):
                    wb = Wh[h % 4]
                    for gi, nm in enumerate(['hq', 'hf', 'hi', 'hg']):
                        if not full and nm in ('hq', 'hg'):
                            continue
                        c0 = OFF[nm] + h * P
                        load_w(stg[:, :, gi * P:(gi + 1) * P], wb[:, :, gi * P:(gi + 1) * P], w_in_v[:, :, c0:c0 + P],
                               ('Wh', h % 4, gi), ('stg', gi))

                def do_block(h, j, fb, wb, wk):
                    hp = h % 2
                    ktok = ktokL[hp]
                    S32 = S32L[hp]
                    Stmp = StmpL[hp]
                    dec = decL[hp]
                    Sbf = SbfL[hp]
                    ATm = ATmL[hp]
                    junk = junkL[hp]
                    sso = ssoL[hp]
                    rso = rsoL[hp]
                    utok = utokL[hp]
                    e_t = e_tL[fb]
                    L1 = L1L[fb]
                    g_t = g_tL[fb]
                    b_t = b_tL[fb]
                    t_t = t_tL[fb]
                    ktT = ktTL[fb]
                    vbf = vbfL[fb]
                    eq = eqL[fb]
                    qtT = qtTL[fb]
                    gate = gateL[fb]
                    tok0 = P + j * 512
                    rk = xkb(1 + 4 * j, 5 + 4 * j)
                    for kc in range(KC):
                        S.op('pe', lambda e, kc=kc, wb=wb, tok0=tok0: e.matmul(ps[0][:], lhsT=wb[:, kc, P:2 * P],
                                                                            rhs=xnT[:, kc, tok0:tok0 + 512],
                                                                            start=(kc == 0), stop=(kc == KC - 1)),
                             reads=rk + [wk[1]], writes=[PK(0)])
                    if full:
                        for kc in range(KC):
                            S.op('pe', lambda e, kc=kc, wb=wb, tok0=tok0: e.matmul(ps[1][:], lhsT=wb[:, kc, 0:P],
                                                                                rhs=xnT[:, kc, tok0:tok0 + 512],
                                                                                start=(kc == 0), stop=(kc == KC - 1)),
                                 reads=rk + [wk[0]], writes=[PK(1)])
                    for tl in range(4):
                        c0 = tok0 + tl * P
                        bank = 2 + tl // 2
                        half = tl % 2
                        if full:
                            o_ap = ps[bank][:, half * 256:(half + 1) * 256]
                            r_sl = slice(2 * P, 4 * P)
                            rkk = [wk[2], wk[3]]
                        else:
                            o_ap = ps[bank][:, half * 256:half * 256 + P]
                            r_sl = slice(2 * P, 3 * P)
                            rkk = [wk[2]]
                        for kc in range(KC):
                            S.op('pe', lambda e, kc=kc, wb=wb, c0=c0, o_ap=o_ap, r_sl=r_sl: e.matmul(
                                o_ap, lhsT=xnT[:, kc, c0:c0 + P], rhs=wb[:, kc, r_sl],
                                start=(kc == 0), stop=(kc == KC - 1)),
                                 reads=xk(1 + 4 * j + tl) + rkk, writes=[PK(bank)])
                    S.op('act', lambda e: e.activation(out=e_t[:], in_=ps[0][:], func=AF.Exp, scale=-1.0),
                         reads=[PK(0)], writes=[('e_t', fb)])
                    S.op('act', lambda e: e.activation(out=L1[:], in_=e_t[:], func=AF.Ln, bias=1.0),
                         reads=[('e_t', fb)], writes=[('L1', fb)])
                    S.op('act', lambda e, h=h: e.activation(out=g_t[:], in_=e_t[:], func=AF.Ln, scale=lb[:, h:h + 1],
                                                        bias=1.0),
                         reads=[('e_t', fb), 'lb'], writes=[('g_t', fb)])
                    S.op('dve', lambda e: e.tensor_tensor(out=g_t[:], in0=g_t[:], in1=L1[:], op=ALU.subtract),
                         reads=[('g_t', fb), ('L1', fb)], writes=[('g_t', fb)])
                    for tl in range(4):
                        S.op('dve', lambda e, tl=tl: e.tensor_tensor_scan(out=b_t[:, tl * P:(tl + 1) * P], data0=ones[:],
                                                                     data1=g_t[:, tl * P:(tl + 1) * P], initial=0.0,
                                                                     op0=ALU.mult, op1=ALU.add),
                             reads=[('g_t', fb), 'ones'], writes=[('b_t', fb, tl)])
                    bk = [('b_t', fb, tl) for tl in range(4)]
                    S.op('dve', lambda e: e.scalar_tensor_tensor(out=t_t[:], in0=ps[0][:], scalar=-1.0, in1=L1[:],
                                                              op0=ALU.mult, op1=ALU.subtract),
                         reads=[PK(0), ('L1', fb)], writes=[('t_t', fb)])
                    S.op('dve', lambda e: e.tensor_tensor(out=t_t[:], in0=t_t[:], in1=b_t[:], op=ALU.subtract),
                         reads=[('t_t', fb)] + bk, writes=[('t_t', fb)])
                    S.op('act', lambda e, h=h: e.activation(out=ktT[:], in_=t_t[:], func=AF.Exp,
                                                        bias=ln1mlb[:, h:h + 1]),
                         reads=[('t_t', fb), 'ln1mlb'], writes=[('ktT', fb)])
                    if full:
                        S.op('act', lambda e: e.activation(out=eq[:], in_=ps[1][:], func=AF.Exp, scale=-1.0),
                             reads=[PK(1)], writes=[('eq', fb)])
                        S.op('act', lambda e: e.activation(out=eq[:], in_=eq[:], func=AF.Ln, bias=1.0),
                             reads=[('eq', fb)], writes=[('eq', fb)])
                        S.op('dve', lambda e: e.tensor_tensor(out=eq[:], in0=b_t[:], in1=eq[:], op=ALU.subtract),
                             reads=[('eq', fb)] + bk, writes=[('eq', fb)])
                        S.op('act', lambda e: e.activation(out=eq[:], in_=eq[:], func=AF.Exp), reads=[('eq', fb)], writes=[('eq', fb)])
                        S.op('dve', lambda e: e.tensor_tensor(out=qtT[:], in0=ps[1][:], in1=eq[:], op=ALU.mult),
                             reads=[PK(1), ('eq', fb)], writes=[('qtT', fb)])
                        for bank in (2, 3):
                            gsrc = ps[bank].rearrange("p (t c) -> p t c", t=2)[:, :, P:2 * P]
                            gdst = gate[:, (bank - 2) * 2:(bank - 2) * 2 + 2, :]
                            S.op('act', lambda e, gsrc=gsrc, gdst=gdst: e.activation(out=gdst, in_=gsrc, func=AF.Exp, scale=-1.0),
                                 reads=[PK(bank)], writes=[('gate', fb, bank)])
                            S.op('act', lambda e, gdst=gdst: e.activation(out=gdst, in_=gdst, func=AF.Ln, bias=1.0),
                                 reads=[('gate', fb, bank)], writes=[('gate', fb, bank)])
                            S.op('act', lambda e, gdst=gdst: e.activation(out=gdst, in_=gdst, func=AF.Exp, scale=-1.0),
                                 reads=[('gate', fb, bank)], writes=[('gate', fb, bank)])
                            S.op('dve', lambda e, gsrc=gsrc, gdst=gdst: e.tensor_tensor(out=gdst, in0=gsrc, in1=gdst, op=ALU.mult),
                                 reads=[PK(bank), ('gate', fb, bank)], writes=[('gate', fb, bank)])
                    for bank in (2, 3):
                        vsrc = ps[bank].rearrange("p (t c) -> p t c", t=2)[:, :, 0:P]
                        vdst = vbf[:, (bank - 2) * 2:(bank - 2) * 2 + 2, :]
                        S.op('act', lambda e, vsrc=vsrc, vdst=vdst: e.activation(out=vdst, in_=vsrc, func=AF.Copy),
                             reads=[PK(bank)], writes=[('vbf', fb, bank)])
                    for tl in range(4):
                        ti = 4 * j + tl
                        sl = slice(tl * P, (tl + 1) * P)
                        vk = ('vbf', fb, 2 + tl // 2)
                        v_ap = vbf[:, tl, :]
                        S.op('act', lambda e, tl=tl: e.activation(out=dec[:], in_=b_t[:, tl * P + P - 1:tl * P + P], func=AF.Exp),
                             reads=[('b_t', fb, tl)], writes=[('dec', hp)])
                        kt_ps = ps[6].bitcast(BF16)[:, 0:P]
                        S.op('pe', lambda e, sl=sl, kt_ps=kt_ps: e.transpose(out=kt_ps, in_=ktT[:, sl], identity=ident[:]),
                             reads=[('ktT', fb), 'ident'], writes=[PK(6)])
                        S.op('dve', lambda e, kt_ps=kt_ps: e.tensor_copy(out=ktok[:], in_=kt_ps), reads=[PK(6)], writes=[('ktok', hp)])
                        if full:
                            S.op('pe', lambda e, sl=sl: e.matmul(ps[4][:, 0:P], lhsT=qtT[:, sl], rhs=Sbf[:], start=True, stop=False),
                                 reads=[('qtT', fb), ('Sbf', hp)], writes=[PK(4)])
                            S.op('pe', lambda e, sl=sl: e.matmul(ps[5][:, 0:P], lhsT=ktT[:, sl], rhs=qtT[:, sl], start=True, stop=True),
                                 reads=[('qtT', fb), ('ktT', fb)], writes=[PK(5)])
                            S.op('dve', lambda e: e.tensor_tensor(out=ATm[:], in0=ps[5][:, 0:P], in1=mC[:, 0:P], op=ALU.mult),
                                 reads=[PK(5), 'mC'], writes=[('ATm', hp)])
                            S.op('pe', lambda e, v_ap=v_ap: e.matmul(ps[4][:, 0:P], lhsT=ATm[:], rhs=v_ap, start=False, stop=True),
                                 reads=[('ATm', hp), vk], writes=[PK(4)])
                        S.op('pe', lambda e, v_ap=v_ap: e.matmul(ps[7][:, 0:P], lhsT=ktok[:], rhs=v_ap, start=True, stop=True),
                             reads=[('ktok', hp), vk], writes=[PK(7)])
                        S.op('dve', lambda e: e.tensor_scalar(out=Stmp[:], in0=S32[:], scalar1=dec[:, 0:1], scalar2=None, op0=ALU.mult),
                             reads=[('S32', hp), ('dec', hp)], writes=[('Stmp', hp)])
                        S.op('dve', lambda e: e.scalar_tensor_tensor(out=S32[:], in0=ps[7][:, 0:P], scalar=dec[:, 0:1], in1=Stmp[:],
                                                                  op0=ALU.mult, op1=ALU.add),
                             reads=[PK(7), ('dec', hp), ('Stmp', hp)], writes=[('S32', hp)])
                        if full:
                            S.op('act', lambda e: e.activation(out=junk[:], in_=ps[4][:, 0:P], func=AF.Square, accum_out=sso[:]),
                                 reads=[PK(4)], writes=[('junkH', hp), ('sso', hp)])
                            S.op('act', lambda e: e.activation(out=rso[:], in_=sso[:], func=AF.Ln, scale=1.0 / P, bias=epsb[:]),
                                 reads=[('sso', hp), 'epsb'], writes=[('rso', hp)])
                            S.op('act', lambda e: e.activation(out=rso[:], in_=rso[:], func=AF.Exp, scale=-0.5),
                                 reads=[('rso', hp)], writes=[('rso', hp)])
                            S.op('dve', lambda e, tl=tl: e.scalar_tensor_tensor(out=utok[:], in0=ps[4][:, 0:P], scalar=rso[:, 0:1],
                                                                            in1=gate[:, tl, :], op0=ALU.mult, op1=ALU.mult),
                                 reads=[PK(4), ('rso', hp), ('gate', fb, 2 + tl // 2)], writes=[('utok', hp)])
                            S.op('dve', lambda e: e.tensor_copy(out=Sbf[:], in_=S32[:]), reads=[('S32', hp)], writes=[('Sbf', hp)])
                            ut_ps = ps[6].bitcast(BF16)[:, 0:P]
                            S.op('pe', lambda e, ut_ps=ut_ps: e.transpose(out=ut_ps, in_=utok[:], identity=ident[:]),
                                 reads=[('utok', hp), 'ident'], writes=[PK(6)])
                            S.op('act', lambda e, ut_ps=ut_ps, h=h, ti=ti: e.activation(out=uhT[:, h, ti * P:(ti + 1) * P], in_=ut_ps,
                                                                                   func=AF.Copy, scale=hnw[:, h:h + 1]),
                                 reads=[PK(6), 'hnw'], writes=[('uhT', h, ti)])

                load_head(0)
                load_head(1)
                cnt_ = [0]
                for pr in range(4):
                    for hh_ in (2 * pr + 2, 2 * pr + 3):
                        if hh_ < 8:
                            load_head(hh_)
                    for h in (2 * pr, 2 * pr + 1):
                        hp = h % 2
                        S.op('dve', lambda e, h=h, hp=hp: e.tensor_copy(out=S32L[hp][:], in_=S_in[:, h, :]), reads=[('S_in', h)], writes=[('S32', hp)])
                        S.op('dve', lambda e, h=h, hp=hp: e.tensor_copy(out=SbfL[hp][:], in_=S_in[:, h, :]), reads=[('S_in', h)], writes=[('Sbf', hp)])
                    for j in range(NB):
                        for h in (2 * pr, 2 * pr + 1):
                            do_block(h, j, cnt_[0] % 2, Wh[h % 4], [('Wh', h % 4, gi) for gi in range(4)])
                            cnt_[0] += 1
            S.barrier()

        def summary_alloc(ph):
            return dict(
                e_t=[sb("pe_t%d" % i, [P, 512], F32, ph) for i in range(4)],
                L1=[sb("pL1%d" % i, [P, 512], F32, ph) for i in range(4)],
                g_t=[sb("pg_t%d" % i, [P, 512], F32, ph) for i in range(4)],
                b_t=[sb("pb_t%d" % i, [P, 512], F32, ph) for i in range(4)],
                t_t=[sb("pt_t%d" % i, [P, 512], F32, ph) for i in range(4)],
                ktT=[sb("pktT%d" % i, [P, 512], BF16, ph) for i in range(6)],
                decs=[sb("pdecs%d" % i, [P, 4], F32, ph) for i in range(6)],
                vall=[sb("pvall%d" % i, [P, 4, D], BF16, ph) for i in range(2)],
                ktok=[sb("pktok%d" % i, [P, P], BF16, ph) for i in range(2)],
                tmpS=[sb("ptmpS%d" % i, [P, P], F32, ph) for i in range(2)])

        def summary_pass(Wf, Wi, ctx=None):
            with contextlib.ExitStack() as ph:
                C = ctx if ctx is not None else summary_alloc(ph)
                e_t, L1, g_t, b_t, t_t, ktT, decs, vall, ktok, tmpS = (C[k_] for k_ in
                    ('e_t', 'L1', 'g_t', 'b_t', 't_t', 'ktT', 'decs', 'vall', 'ktok', 'tmpS'))

                def vproj(j):
                    jb = j % 2
                    tok0 = P + j * 512
                    for tl in range(4):
                        c0 = tok0 + tl * P
                        for half in range(2):
                            bank = 2 + (2 * tl + half) % 2
                            for kc in range(KC):
                                S.op('pe', lambda e, kc=kc, c0=c0, half=half, bank=bank: e.matmul(
                                    ps[bank][:], lhsT=xnT[:, kc, c0:c0 + P], rhs=Wi[:, kc, half * 512:(half + 1) * 512],
                                    start=(kc == 0), stop=(kc == KC - 1)),
                                     reads=xk(1 + 4 * j + tl) + [('Wi', half)], writes=[PK(bank)], cost=0.22)
                            S.op('act', lambda e, jb=jb, tl=tl, half=half, bank=bank: e.activation(
                                out=vall[jb][:, tl, half * 512:(half + 1) * 512], in_=ps[bank][:], func=AF.Copy),
                                 reads=[PK(bank)], writes=[('vall', jb, tl, half)])

                def front(idx, j, h):
                    pb = idx % 2
                    bf = idx % 4
                    hb = idx % 6
                    tok0 = P + j * 512
                    rk = xkb(1 + 4 * j, 5 + 4 * j)
                    for kc in range(KC):
                        S.op('pe', lambda e, kc=kc, tok0=tok0, h=h, pb=pb: e.matmul(
                            ps[pb][:], lhsT=Wf[:, kc, h * P:(h + 1) * P], rhs=xnT[:, kc, tok0:tok0 + 512],
                            start=(kc == 0), stop=(kc == KC - 1)),
                             reads=rk + [('Wf', h // 4)], writes=[PK(pb)], cost=0.22)
                    S.op('act', lambda e, bf=bf, pb=pb: e.activation(out=e_t[bf][:], in_=ps[pb][:], func=AF.Exp, scale=-1.0),
                         reads=[PK(pb)], writes=[('e_t', bf)])
                    S.op('dve', lambda e, bf=bf, pb=pb: e.tensor_scalar(out=t_t[bf][:], in0=ps[pb][:], scalar1=-1.0, scalar2=None, op0=ALU.mult),
                         reads=[PK(pb)], writes=[('t_t', bf)])
                    S.op('act', lambda e, bf=bf: e.activation(out=L1[bf][:], in_=e_t[bf][:], func=AF.Ln, bias=1.0),
                         reads=[('e_t', bf)], writes=[('L1', bf)])
                    S.op('act', lambda e, bf=bf, h=h: e.activation(out=g_t[bf][:], in_=e_t[bf][:], func=AF.Ln, scale=lb[:, h:h + 1], bias=1.0),
                         reads=[('e_t', bf), 'lb'], writes=[('g_t', bf)])
                    S.op('dve', lambda e, bf=bf: e.tensor_tensor(out=g_t[bf][:], in0=g_t[bf][:], in1=L1[bf][:], op=ALU.subtract),
                         reads=[('g_t', bf), ('L1', bf)], writes=[('g_t', bf)])
                    for tl in range(4):
                        S.op('dve', lambda e, tl=tl, bf=bf: e.tensor_tensor_scan(out=b_t[bf][:, tl * P:(tl + 1) * P], data0=ones[:],
                                                                            data1=g_t[bf][:, tl * P:(tl + 1) * P], initial=0.0,
                                                                            op0=ALU.mult, op1=ALU.add),
                             reads=[('g_t', bf), 'ones'], writes=[('b_t', bf, tl)])
                    bk = [('b_t', bf, tl) for tl in range(4)]
                    S.op('dve', lambda e, bf=bf: e.tensor_tensor(out=t_t[bf][:], in0=t_t[bf][:], in1=L1[bf][:], op=ALU.subtract),
                         reads=[('t_t', bf), ('L1', bf)], writes=[('t_t', bf)])
                    S.op('dve', lambda e, bf=bf: e.tensor_tensor(out=t_t[bf][:], in0=t_t[bf][:], in1=b_t[bf][:], op=ALU.subtract),
                         reads=[('t_t', bf)] + bk, writes=[('t_t', bf)])
                    S.op('act', lambda e, bf=bf, hb=hb, h=h: e.activation(out=ktT[hb][:], in_=t_t[bf][:], func=AF.Exp, bias=ln1mlb[:, h:h + 1]),
                         reads=[('t_t', bf), 'ln1mlb'], writes=[('ktT', hb)])
                    blast = b_t[bf].rearrange("p (t c) -> p t c", t=4)[:, :, P - 1]
                    S.op('act', lambda e, bf=bf, hb=hb, blast=blast: e.activation(out=decs[hb][:], in_=blast, func=AF.Exp),
                         reads=bk, writes=[('decs', hb)])

                def back(idx, j, h):
                    bf = idx % 2
                    hb = idx % 6
                    jb = j % 2
                    for tl in range(4):
                        tb = tl % 2
                        kt_ps = ps[4 + tb].bitcast(BF16)[:, 0:P]
                        S.op('pe', lambda e, tl=tl, hb=hb, kt_ps=kt_ps: e.transpose(out=kt_ps, in_=ktT[hb][:, tl * P:(tl + 1) * P], identity=ident[:]),
                             reads=[('ktT', hb), 'ident'], writes=[PK(4 + tb)])
                        if tb == 0:
                            S.op('dve', lambda e, tb=tb, kt_ps=kt_ps: e.tensor_copy(out=ktok[tb][:], in_=kt_ps), reads=[PK(4 + tb)], writes=[('ktok', tb)])
                        else:
                            S.op('act', lambda e, tb=tb, kt_ps=kt_ps: e.activation(out=ktok[tb][:], in_=kt_ps, func=AF.Copy), reads=[PK(4 + tb)], writes=[('ktok', tb)])
                        S.op('pe', lambda e, tl=tl, tb=tb, jb=jb, h=h: e.matmul(ps[6 + tb][:, 0:P], lhsT=ktok[tb][:], rhs=vall[jb][:, tl, h * P:(h + 1) * P],
                                                                          start=True, stop=True),
                             reads=[('ktok', tb), ('vall', jb, tl, h // 4)], writes=[PK(6 + tb)])
                        S.op('act', lambda e, tl=tl, tb=tb, hb=hb: e.activation(out=tmpS[tb][:], in_=ps[6 + tb][:, 0:P], func=AF.Copy,
                                                                          scale=decs[hb][:, tl:tl + 1]),
                             reads=[PK(6 + tb), ('decs', hb)], writes=[('tmpS', tb)])
                        S.op('dve', lambda e, tl=tl, tb=tb, hb=hb, h=h: e.scalar_tensor_tensor(out=S_in[:, h, :], in0=S_in[:, h, :], scalar=decs[hb][:, tl:tl + 1],
                                                                                     in1=tmpS[tb][:], op0=ALU.mult, op1=ALU.add),
                             reads=[('tmpS', tb), ('decs', hb), ('S_in', h)], writes=[('S_in', h)])

                seq = [(j, h) for j in range(NB) for h in range(8)]
                vproj(0)
                front(0, *seq[0])
                for idx, (j, h) in enumerate(seq):
                    if idx + 1 < len(seq):
                        nj, nh = seq[idx + 1]
                        if nh == 0:
                            vproj(nj)
                        front(idx + 1, nj, nh)
                    back(idx, j, h)
            if ctx is None:
                S.barrier()

        with contextlib.ExitStack() as pre:
            stg1 = sb("stg1", [P, KC, 512], F32, pre)
            Wf = sb("Wf", [P, KC, D], BF16, pre)
            Wi = sb("Wi", [P, KC, D], BF16, pre)
            for hh in range(2):
                load_w(stg1[:], Wf[:, :, hh * 512:(hh + 1) * 512], w_in_v[:, :, OFF['hf'] + hh * 512:OFF['hf'] + (hh + 1) * 512],
                       ('Wf', hh), 'stg1')
                load_w(stg1[:], Wi[:, :, hh * 512:(hh + 1) * 512], w_in_v[:, :, OFF['hi'] + hh * 512:OFF['hi'] + (hh + 1) * 512],
                       ('Wi', hh), 'stg1')
            actx = stage_a_alloc(pre)
            sctx = summary_alloc(pre)
            for k in range(NPREV):
                stage_a([(1 + i, x_prev[(k * NT + i) * P:(k * NT + i + 1) * P, :]) for i in range(NT)], actx)
                summary_pass(Wf, Wi, sctx)
            stage_a([(0, x_halo)] + [(1 + i, x_own[i * P:(i + 1) * P, :]) for i in range(NT)], actx)
        S.barrier()
        uhT = sb("uhT", [P, KC, T], BF16)
        uaT = sb("uaT", [P, KC, T], BF16)

        with contextlib.ExitStack() as ph:
            stg = sb("stgA", [P, KC, 704], F32, ph)
            Wa = [sb("Wa%d" % i, [P, KC, 704], BF16, ph) for i in range(2)]
            kT = sb("kT", [P, TA], BF16, ph)
            qT = sb("qT", [P, 2, 2, T], BF16, ph)
            S.op('pool', lambda e: e.memset(qT[:], 0.0), writes=['qTz'])
            vext = sb("vext", [P, NT + 1, 65], BF16, ph)
            ga = sb("ga", [P, NT, 256], BF16, ph)
            gtmp = sb("gtmp", [P, 256], F32, ph)
            PmL = [[sb("Pm%d_%d" % (q_, i), [P, 512], BF16, ph) for i in range(2)] for q_ in range(2)]
            denL = [sb("den%d" % q_, [P, 4], F32, ph) for q_ in range(2)]
            uatL = [sb("uat%d" % q_, [P, 256], BF16, ph) for q_ in range(2)]

            def load_grp(g):
                wb = Wa[g % 2]
                parts = [(0, 256, OFF['aq'] + g * 256, 0), (256, 64, OFF['ak'] + g * 64, 1), (320, 64, OFF['ak'] + g * 64, 2),
                         (384, 64, OFF['av'] + g * 64, 3), (448, 256, OFF['ag'] + g * 256, 4)]
                for (d0, n, c0, pi) in parts:
                    load_w(stg[:, :, d0:d0 + n], wb[:, :, d0:d0 + n], w_in_v[:, :, c0:c0 + n], ('Wa', g % 2, pi), ('stgA', pi))

            load_grp(0)
            for g in range(4):
                if g + 1 < 4:
                    load_grp(g + 1)
                wb = Wa[g % 2]
                wk = [('Wa', g % 2, pi) for pi in range(5)]
                S.op('dve', lambda e: e.memset(vext[:, :, 64:65], 1.0), writes=[('vext1',)])
                c0 = 0
                while c0 < TA:
                    n = min(512, TA - c0)
                    for kc in range(KC):
                        S.op('pe', lambda e, kc=kc, c0=c0, n=n, wb=wb: e.matmul(ps[0][:, 0:n], lhsT=wb[:, kc, 256:384],
                                                                             rhs=xnT[:, kc, c0:c0 + n], start=(kc == 0), stop=(kc == KC - 1)),
                             reads=xkb(c0 // P, (c0 + n) // P) + [wk[1], wk[2]], writes=[PK(0)])
                    S.op('act', lambda e, c0=c0, n=n: e.activation(out=kT[:, c0:c0 + n], in_=ps[0][:, 0:n], func=AF.Copy),
                         reads=[PK(0)], writes=[('kT', c0 // 512)])
                    c0 += n
                for j in range(NB):
                    tok0 = P + j * 512
                    for ch in range(2):
                        for kc in range(KC):
                            S.op('pe', lambda e, kc=kc, ch=ch, tok0=tok0, wb=wb: e.matmul(ps[1][:], lhsT=wb[:, kc, ch * P:(ch + 1) * P],
                                                                                       rhs=xnT[:, kc, tok0:tok0 + 512],
                                                                                       start=(kc == 0), stop=(kc == KC - 1)),
                                 reads=xkb(1 + 4 * j, 5 + 4 * j) + [wk[0]], writes=[PK(1)])
                        S.op('dve', lambda e, ch=ch, j=j: e.tensor_copy(out=qT[0:64, 0, ch, j * 512:(j + 1) * 512], in_=ps[1][0:64, :]),
                             reads=[PK(1), 'qTz'], writes=[('qT', j, ch, 0)])
                        S.op('act', lambda e, ch=ch, j=j: e.activation(out=qT[64:128, 1, ch, j * 512:(j + 1) * 512], in_=ps[1][64:128, :],
                                                                   func=AF.Copy),
                             reads=[PK(1), 'qTz'], writes=[('qT', j, ch, 1)])
                for ti in range(NT + 1):
                    for kc in range(KC):
                        S.op('pe', lambda e, kc=kc, ti=ti, wb=wb: e.matmul(ps[2][:, 0:64], lhsT=xnT[:, kc, ti * P:(ti + 1) * P],
                                                                        rhs=wb[:, kc, 384:448], start=(kc == 0), stop=(kc == KC - 1)),
                             reads=xk(ti) + [wk[3]], writes=[PK(2)])
                    S.op('act', lambda e, ti=ti: e.activation(out=vext[:, ti, 0:64], in_=ps[2][:, 0:64], func=AF.Copy),
                         reads=[PK(2)], writes=[('vext', ti)])
                    if ti >= 1:
                        for kc in range(KC):
                            S.op('pe', lambda e, kc=kc, ti=ti, wb=wb: e.matmul(ps[3][:, 0:256], lhsT=xnT[:, kc, ti * P:(ti + 1) * P],
                                                                            rhs=wb[:, kc, 448:704], start=(kc == 0), stop=(kc == KC - 1)),
                                 reads=xk(ti) + [wk[4]], writes=[PK(3)])
                        S.op('act', lambda e: e.activation(out=gtmp[:], in_=ps[3][:, 0:256], func=AF.Exp, scale=-1.0),
                             reads=[PK(3)], writes=['gtmp'])
                        S.op('act', lambda e: e.activation(out=gtmp[:], in_=gtmp[:], func=AF.Ln, bias=1.0), reads=['gtmp'], writes=['gtmp'])
                        S.op('act', lambda e: e.activation(out=gtmp[:], in_=gtmp[:], func=AF.Exp, scale=-1.0), reads=['gtmp'], writes=['gtmp'])
                        S.op('dve', lambda e, ti=ti: e.tensor_tensor(out=ga[:, ti - 1, :], in0=ps[3][:, 0:256], in1=gtmp[:], op=ALU.mult),
                             reads=[PK(3), 'gtmp'], writes=[('ga', ti - 1)])
                def attn_tile(g, ti):
                    par = ti % 2
                    Pm = PmL[par]
                    den = denL[par]
                    uat = uatL[par]
                    q_r = [('qT', ti // 4, c_, h_) for c_ in range(2) for h_ in range(2)]
                    for kb in range(2):
                        kcol = (ti + kb) * P
                        bank = 4 + kb
                        for hp in range(2):
                            o_ap = ps[bank][:, hp * 256:(hp + 1) * 256]
                            S.op('pe', lambda e, hp=hp, kcol=kcol, o_ap=o_ap, ti=ti: e.matmul(
                                o_ap, lhsT=kT[:, kcol:kcol + P],
                                rhs=qT[:, hp, :, ti * P:(ti + 1) * P], start=True, stop=True),
                                 reads=q_r + [('kT', kcol // 512)], writes=[PK(bank)])
                        S.op('act', lambda e, kb=kb, bank=bank: e.activation(out=Pm[kb][:], in_=ps[bank][:], func=AF.Exp, scale=0.125),
                             reads=[PK(bank)], writes=[('Pm', par, kb)])
                        msk = mC if kb == 1 else (mP0 if ti == 0 else mP)
                        S.op('dve', lambda e, kb=kb, msk=msk: e.tensor_tensor(out=Pm[kb][:], in0=Pm[kb][:], in1=msk[:], op=ALU.mult),
                             reads=[('Pm', par, kb), 'mC', 'mP', 'mP0'], writes=[('Pm', par, kb)])
                    pso = ps[6].rearrange("p (a t) -> p a t", a=4)
                    for a in range(4):
                        for kb in range(2):
                            S.op('pe', lambda e, a=a, kb=kb, ti=ti, pso=pso: e.matmul(pso[:, a, 0:65], lhsT=Pm[kb][:, ((a % 2) * 2 + a // 2) * P:((a % 2) * 2 + a // 2 + 1) * P],
                                                                                   rhs=vext[:, ti + kb, :], start=(kb == 0), stop=(kb == 1)),
                                 reads=[('Pm', par, kb), ('vext', ti + kb), ('vext1',)], writes=[PK(6)])
                    S.op('dve', lambda e, pso=pso, g=g: e.tensor_tensor(out=den[:], in0=pso[:, :, 64], in1=esink[:, 4 * g:4 * g + 4], op=ALU.add),
                         reads=[PK(6), 'esink'], writes=[('den', par)])
                    S.op('dve', lambda e: e.reciprocal(out=den[:], in_=den[:]), reads=[('den', par)], writes=[('den', par)])
                    for a in range(4):
                        S.op('dve', lambda e, a=a, pso=pso, ti=ti: e.scalar_tensor_tensor(
                            out=uat[:, a * 64:(a + 1) * 64], in0=pso[:, a, 0:64], scalar=den[:, a:a + 1],
                            in1=ga[:, ti, a * 64:(a + 1) * 64], op0=ALU.mult, op1=ALU.mult),
                             reads=[PK(6), ('den', par), ('ga', ti)], writes=[('uat', par, a)])
                    ut_ps = ps[7].bitcast(BF16).rearrange("p (c t) -> p c t", c=8)
                    for cc in range(2):
                        S.op('pe', lambda e, cc=cc, ut_ps=ut_ps: e.transpose(out=ut_ps[:, cc, :], in_=uat[:, cc * P:(cc + 1) * P], identity=ident[:]),
                             reads=[('uat', par, a) for a in range(4)] + ['ident'], writes=[PK(7)])
                    S.op('act', lambda e, ut_ps=ut_ps, g=g, ti=ti: e.activation(out=uaT[:, 2 * g:2 * g + 2, ti * P:(ti + 1) * P],
                                                                           in_=ut_ps[:, 0:2, :], func=AF.Copy),
                         reads=[PK(7)], writes=[('uaT', g, ti)])
                for ti in range(NT):
                    attn_tile(g, ti)
        S.barrier()
        hgrn_phase(full=True)

        with contextlib.ExitStack() as ph:
            stg = sb("stgP", [P, KC, P], F32, ph)
            Wc = [sb("Wc%d" % i, [P, KC, 512], BF16, ph) for i in range(2)]
            Wo = sb("Wo", [P, KC, D], BF16, ph)
            mT = sb("mT", [P, KC, T], BF16, ph)
            sgL = [[sb("sg%d_%d" % (q_, i), [P, 512], F32, ph) for i in range(2)] for q_ in range(2)]
            m1L = [sb("m1_%d" % q_, [P, 512], F32, ph) for q_ in range(2)]
            xt = [sb("xtP%d" % i, [P, D], F32, ph) for i in range(2)]
            junk = sb("junkP", [P, D], BF16, ph)
            ss2 = sb("ss2", [P, 1], F32, ph)
            rs2 = sb("rs2", [P, 1], F32, ph)
            fnw = sb("fnw", [P, D], F32, ph)
            S.dma(fnw[:], fnw_d, writes=['fnw'])
            w_bh_v = w_bh.rearrange("(kc p) n -> p kc n", p=P)
            w_ba_v = w_ba.rearrange("(kc p) n -> p kc n", p=P)
            w_out_v = w_out.rearrange("(kc p) n -> p kc n", p=P)

            def load_chunk(mc):
                wb = Wc[mc % 2]
                srcs = [w_bh_v[:, :, mc * P:(mc + 1) * P], w_ba_v[:, :, mc * P:(mc + 1) * P],
                        w_in_v[:, :, OFF['mh'] + mc * P:OFF['mh'] + (mc + 1) * P],
                        w_in_v[:, :, OFF['ma'] + mc * P:OFF['ma'] + (mc + 1) * P]]
                for pi, s_ap in enumerate(srcs):
                    load_w(stg[:], wb[:, :, pi * P:(pi + 1) * P], s_ap, ('Wc', mc % 2, pi), 'stgP')

            def post_item(mc, j, wb, wk):
                par = (mc * NB + j) % 2
                pb = 4 * par
                sg = sgL[par]
                m1 = m1L[par]
                tok0 = P + j * 512
                rk = xkb(1 + 4 * j, 5 + 4 * j)
                for kc in range(KC):
                    S.op('pe', lambda e, kc=kc, wb=wb, j=j: e.matmul(ps[pb][:], lhsT=wb[:, kc, 0:P],
                                                                  rhs=uhT[:, kc, j * 512:(j + 1) * 512], start=(kc == 0), stop=(kc == KC - 1)),
                         reads=[wk[0]], writes=[PK(pb)])
                for kc in range(KC):
                    S.op('pe', lambda e, kc=kc, wb=wb, j=j: e.matmul(ps[pb + 1][:], lhsT=wb[:, kc, P:2 * P],
                                                                  rhs=uaT[:, kc, j * 512:(j + 1) * 512], start=(kc == 0), stop=(kc == KC - 1)),
                         reads=[wk[1]], writes=[PK(pb + 1)])
                for w in range(2):
                    for kc in range(KC):
                        S.op('pe', lambda e, kc=kc, wb=wb, w=w, tok0=tok0: e.matmul(
                            ps[pb + 2 + w][:], lhsT=wb[:, kc, (2 + w) * P:(3 + w) * P],
                            rhs=xnT[:, kc, tok0:tok0 + 512], start=(kc == 0), stop=(kc == KC - 1)),
                             reads=rk + [wk[2 + w]], writes=[PK(pb + 2 + w)])
                    S.op('act', lambda e, w=w: e.activation(out=sg[w][:], in_=ps[pb + 2 + w][:], func=AF.Exp, scale=-1.0),
                         reads=[PK(pb + 2 + w)], writes=[('sg', par, w)])
                    S.op('act', lambda e, w=w: e.activation(out=sg[w][:], in_=sg[w][:], func=AF.Ln, bias=1.0),
                         reads=[('sg', par, w)], writes=[('sg', par, w)])
                    S.op('act', lambda e, w=w: e.activation(out=sg[w][:], in_=sg[w][:], func=AF.Exp, scale=-1.0),
                         reads=[('sg', par, w)], writes=[('sg', par, w)])
                S.op('dve', lambda e: e.tensor_tensor(out=m1[:], in0=ps[pb][:], in1=sg[0][:], op=ALU.mult),
                     reads=[PK(pb), ('sg', par, 0)], writes=[('m1', par)])
                S.op('dve', lambda e: e.tensor_tensor(out=sg[1][:], in0=ps[pb + 1][:], in1=sg[1][:], op=ALU.mult),
                     reads=[PK(pb + 1), ('sg', par, 1)], writes=[('sg', par, 1)])
                S.op('dve', lambda e, mc=mc, j=j: e.tensor_tensor(out=mT[:, mc, j * 512:(j + 1) * 512], in0=m1[:], in1=sg[1][:], op=ALU.add),
                     reads=[('m1', par), ('sg', par, 1)], writes=[('mT', mc, j)])

            load_chunk(0)
            for mc in range(KC):
                if mc + 1 < KC:
                    load_chunk(mc + 1)
                wb = Wc[mc % 2]
                wk = [('Wc', mc % 2, pi) for pi in range(4)]
                for j in range(NB):
                    post_item(mc, j, wb, wk)

            for q in range(KC):
                load_w(stg[:], Wo[:, :, q * P:(q + 1) * P], w_out_v[:, :, q * P:(q + 1) * P], ('Wo', q // 4), 'stgP')
            for ti in range(NT):
                j = ti // 4
                b = ti % 2
                S.dma(xt[b][:], x_own[ti * P:(ti + 1) * P, :], writes=[('xtP', b)])
                for nh in range(2):
                    bank = 4 + (2 * ti + nh) % 4
                    for mc in range(KC):
                        S.op('pe', lambda e, mc=mc, nh=nh, ti=ti, bank=bank: e.matmul(
                            ps[bank][:], lhsT=mT[:, mc, ti * P:(ti + 1) * P], rhs=Wo[:, mc, nh * 512:(nh + 1) * 512],
                            start=(mc == 0), stop=(mc == KC - 1)),
                             reads=[('mT', m, j) for m in range(KC)] + [('Wo', nh)], writes=[PK(bank)])
                    S.op('dve', lambda e, nh=nh, b=b, bank=bank: e.tensor_tensor(out=xt[b][:, nh * 512:(nh + 1) * 512], in0=ps[bank][:],
                                                                            in1=xt[b][:, nh * 512:(nh + 1) * 512], op=ALU.add),
                         reads=[PK(bank), ('xtP', b)], writes=[('xtP', b)])
                S.op('act', lambda e, b=b: e.activation(out=junk[:], in_=xt[b][:], func=AF.Square, accum_out=ss2[:]),
                     reads=[('xtP', b)], writes=['junkP', 'ss2'])
                S.op('act', lambda e: e.activation(out=rs2[:], in_=ss2[:], func=AF.Ln, scale=1.0 / D, bias=epsb[:]),
                     reads=['ss2', 'epsb'], writes=['rs2'])
                S.op('act', lambda e: e.activation(out=rs2[:], in_=rs2[:], func=AF.Exp, scale=-0.5), reads=['rs2'], writes=['rs2'])
                S.op('dve', lambda e, b=b: e.scalar_tensor_tensor(out=xt[b][:], in0=xt[b][:], scalar=rs2[:, 0:1], in1=fnw[:],
                                                              op0=ALU.mult, op1=ALU.mult),
                     reads=[('xtP', b), 'rs2', 'fnw'], writes=[('xtP', b)])
                S.dma(out_d[ti * P:(ti + 1) * P, :], xt[b][:], reads=[('xtP', b)], writes=[('outst', ti)])
        S.finish('sp')
        S.barrier()
    return nc


def _host_inputs(x, norm_w, w_in, hgrn_lower_bound, hgrn_norm_w, w_branch_hgrn, attn_sinks,
                 w_branch_attn, w_out, final_norm_w, NT):
    f32 = np.float32
    bf = ml_dtypes.bfloat16
    x = np.asarray(x, f32)
    B, SEQ, _ = x.shape
    T = NT * P
    nseg = SEQ // T
    w_in0 = np.ascontiguousarray(np.asarray(w_in, f32)[0])
    wbh = np.ascontiguousarray(np.asarray(w_branch_hgrn, f32)[0])
    wba = np.ascontiguousarray(np.asarray(w_branch_attn, f32)[0])
    wo = np.ascontiguousarray(np.asarray(w_out, f32)[0])
    nw = np.ascontiguousarray(np.asarray(norm_w, f32)[0].reshape(KC, P).T)
    lbp = np.asarray(hgrn_lower_bound, f32)
    lbp = np.ascontiguousarray(np.concatenate([lbp[0].reshape(8, P).T, lbp[1].reshape(8, P).T], axis=1))
    hnw = np.ascontiguousarray(np.asarray(hgrn_norm_w, f32)[0].reshape(8, P).T)
    sinks = np.ascontiguousarray(np.broadcast_to(np.asarray(attn_sinks, f32)[0][None, :], (P, 16)))
    fnw = np.ascontiguousarray(np.broadcast_to(np.asarray(final_norm_w, f32)[None, :], (P, D)))
    ident = np.eye(P, dtype=f32).astype(bf)
    si = np.arange(P)[:, None]
    tj = np.arange(P)[None, :]
    mC = np.tile((si <= tj).astype(f32), (1, 4)).astype(bf)
    mPm = np.tile((si > tj).astype(f32), (1, 4)).astype(bf)
    zeros_m = np.zeros((P, 512), f32).astype(bf)
    in_maps = []
    ncores = B * nseg
    for c in range(ncores):
        b, s = divmod(c, nseg)
        xo = np.ascontiguousarray(x[b, s * T:(s + 1) * T, :])
        xh = np.ascontiguousarray(x[b, s * T - P:s * T, :]) if s > 0 else np.zeros((P, D), f32)
        xp = np.zeros((NPREV * T, D), f32)
        for k in range(NPREV):
            sp = s - NPREV + k
            if sp >= 0:
                xp[k * T:(k + 1) * T] = x[b, sp * T:(sp + 1) * T, :]
        in_maps.append(dict(x_own=xo, x_halo=xh, w_in=w_in0, w_bh=wbh, w_ba=wba, w_out=wo, nw=nw, lbp=lbp, hnw=hnw,
                            sinks=sinks, fnw=fnw, x_prev=xp, ident=ident, maskC=mC, maskP=mPm,
                            maskP0=(zeros_m if s == 0 else mPm)))
    return in_maps, B, nseg, T


def kernel(x, norm_w, w_in, hgrn_lower_bound, hgrn_norm_w, w_branch_hgrn, attn_sinks,
           w_branch_attn, w_out, final_norm_w):
    NT = 16
    in_maps, B, nseg, T = _host_inputs(x, norm_w, w_in, hgrn_lower_bound, hgrn_norm_w, w_branch_hgrn, attn_sinks,
                                       w_branch_attn, w_out, final_norm_w, NT)
    nc = build_nc(NT)
    res = run_bass_kernel_spmd(nc, in_maps, core_ids=list(range(NCORES)))
    out = np.zeros((B, nseg * T, D), np.float32)
    for c in range(NCORES):
        b, s = divmod(c, nseg)
        out[b, s * T:(s + 1) * T, :] = np.asarray(res.results[c]["out"], np.float32)
    return out
```
